# Optimizing a Trainium2 kernel written in Bass

```python
import math
import jax, jax.numpy as jnp
from jax import lax
import numpy as np

D_MODEL = 1024
BATCH = 16
SEQ = 256
DEPTH = 4
DEC_BATCH = 2
DEC_SEQ = 2048
PAST_LEN = 256

GRID_W = 64
BRANCH_W = 512
N_BRANCH = 3
SSM_P = 16
SSM_G = BRANCH_W // SSM_P
SSM_N = 64
N_DIR = 2
FOURIER_GROUPS = 4
FOURIER_C = BRANCH_W // FOURIER_GROUPS
HEAD_DIM = 64
N_HEADS = BRANCH_W // HEAD_DIM
N_KV = 2
KV_GROUP = N_HEADS // N_KV
KV_W = N_KV * HEAD_DIM
Q_BLOCK = 128
ROPE_THETA = 10000.0
EPS = 1e-6
DT_MIN = 1e-3
DT_MAX = 1e-1
SPLITS = (BRANCH_W, 2 * BRANCH_W, 3 * BRANCH_W, 4 * BRANCH_W, 5 * BRANCH_W,
          5 * BRANCH_W + KV_W, 5 * BRANCH_W + 2 * KV_W)
IN_WIDTH = 6 * BRANCH_W + 2 * KV_W

kernel_name = 'hybrid_s5_fnet_gqa_diffusion_step'


def rmsnorm(x, g):
    x32 = x.astype(jnp.float32)
    y = x32 * lax.rsqrt(jnp.mean(x32 * x32, axis=-1, keepdims=True) + EPS)
    return (y * g.astype(jnp.float32)).astype(x.dtype)


def adaln(cvec, w, b):
    m = jax.nn.silu(cvec) @ w + b
    shift, scale, gate = jnp.split(m[:, None, :], 3, axis=-1)
    return shift, scale, gate


def axial_rope(seq):
    rows = seq // GRID_W
    row = jnp.repeat(jnp.arange(rows), GRID_W).astype(jnp.float32)
    col = jnp.tile(jnp.arange(GRID_W), rows).astype(jnp.float32)
    n_freq = HEAD_DIM // 4
    inv = ROPE_THETA ** (-jnp.arange(n_freq, dtype=jnp.float32) / n_freq)
    ang = jnp.concatenate([row[:, None] * inv, col[:, None] * inv], axis=-1)
    return jnp.cos(ang), jnp.sin(ang)


def apply_rope(x, cos, sin):
    x32 = x.astype(jnp.float32)
    x1, x2 = x32[..., 0::2], x32[..., 1::2]
    c, s = cos[None, :, None, :], sin[None, :, None, :]
    out = jnp.stack([x1 * c - x2 * s, x1 * s + x2 * c], axis=-1)
    return out.reshape(x.shape).astype(x.dtype)


def blocked_attention(q, k, v):
    b, s = q.shape[0], q.shape[1]
    nblk = s // Q_BLOCK
    qb = q.reshape(b, nblk, Q_BLOCK, N_KV, KV_GROUP, HEAD_DIM).transpose(1, 0, 2, 3, 4, 5)
    scale = HEAD_DIM ** -0.5

    def one_block(qi):
        sc = jnp.einsum('bqkgd,bskd->bkgqs', qi, k).astype(jnp.float32) * scale
        p = jax.nn.softmax(sc, axis=-1).astype(v.dtype)
        return jnp.einsum('bkgqs,bskd->bqkgd', p, v)

    o = lax.map(one_block, qb)
    return o.transpose(1, 0, 2, 3, 4, 5).reshape(b, s, N_HEADS * HEAD_DIM)


def s5_discretise(a_re, a_im, log_dt, b_re, b_im):
    lam = lax.complex(a_re.astype(jnp.float32), a_im.astype(jnp.float32))
    dt = jnp.exp(log_dt.astype(jnp.float32))[:, None]
    a_bar = jnp.exp(lam * dt)
    b_mat = lax.complex(b_re.astype(jnp.float32), b_im.astype(jnp.float32))
    b_bar = ((a_bar - 1.0) / lam)[..., None] * b_mat
    return a_bar, b_bar


def s5_scan(u32, a_bar, b_bar, h0, reverse):
    bu = jnp.einsum('bsgp,gnp->bsgn', u32.astype(jnp.complex64), b_bar)
    if h0 is not None:
        first = -1 if reverse else 0
        bu = bu.at[:, first].add(a_bar * h0)
    a = jnp.broadcast_to(a_bar, bu.shape)

    def combine(e1, e2):
        a1, b1 = e1
        a2, b2 = e2
        return a1 * a2, a2 * b1 + b2

    _, hs = lax.associative_scan(combine, (a, bu), axis=1, reverse=reverse)
    return hs


def s5_mixer(u, a_re, a_im, log_dt, b_re, b_im, c_re, c_im, d_skip, h0_f, h0_b):
    b, s, _ = u.shape
    u32 = u.astype(jnp.float32).reshape(b, s, SSM_G, SSM_P)
    y = d_skip.astype(jnp.float32).reshape(SSM_G, SSM_P) * u32
    finals = []
    for d, (h0, rev) in enumerate(((h0_f, False), (h0_b, True))):
        a_bar, b_bar = s5_discretise(a_re[d], a_im[d], log_dt[d], b_re[d], b_im[d])
        hs = s5_scan(u32, a_bar, b_bar, h0, rev)
        c_mat = lax.complex(c_re[d].astype(jnp.float32), c_im[d].astype(jnp.float32))
        y = y + jnp.real(jnp.einsum('bsgn,gpn->bsgp', hs, c_mat))
        finals.append(hs[:, 0] if rev else hs[:, -1])
    return y.reshape(b, s, BRANCH_W).astype(u.dtype), finals


def fourier_mix(u):
    b, s, _ = u.shape
    u32 = u.astype(jnp.float32).reshape(b, s, FOURIER_GROUPS, FOURIER_C)
    z = jnp.real(jnp.fft.fft2(u32, axes=(1, 3), norm='ortho'))
    return z.reshape(b, s, BRANCH_W).astype(u.dtype)


def mixer_layer(x, cvec, lw, ctx_kv, ssm_h0, rope):
    (norm_g, ada_w, ada_b, w_in, a_re, a_im, log_dt, b_re, b_im, c_re, c_im, d_skip,
     glu_w, fnet_w, q_gain, k_gain, w_branch, merge_w, merge_b, w_out) = lw
    b, s, _ = x.shape
    shift, scale, gate = adaln(cvec, ada_w, ada_b)
    h = rmsnorm(x, norm_g) * (1.0 + scale) + shift
    a_in, a_gate, f_in, f_gate, q, k, v, c_gate = jnp.split(h @ w_in, SPLITS, axis=-1)
    h0_f, h0_b = (None, None) if ssm_h0 is None else ssm_h0
    ya, finals = s5_mixer(a_in, a_re, a_im, log_dt, b_re, b_im, c_re, c_im, d_skip, h0_f, h0_b)
    za = jax.nn.gelu(ya) @ glu_w
    ya = za[..., :BRANCH_W] * jax.nn.sigmoid(za[..., BRANCH_W:])
    yf = fourier_mix(f_in) @ fnet_w
    q = rmsnorm(q.reshape(b, s, N_HEADS, HEAD_DIM), q_gain)
    k = rmsnorm(k.reshape(b, s, N_KV, HEAD_DIM), k_gain)
    v = v.reshape(b, s, N_KV, HEAD_DIM)
    if rope is None:
        yc = blocked_attention(q, k, v)
    else:
        cos, sin = rope
        k_all = jnp.concatenate([apply_rope(k, cos, sin), ctx_kv[0]], axis=1)
        v_all = jnp.concatenate([v, ctx_kv[1]], axis=1)
        yc = blocked_attention(apply_rope(q, cos, sin), k_all, v_all)
    branches = jnp.stack([ya * jax.nn.silu(a_gate), yf * jax.nn.silu(f_gate),
                          yc * jax.nn.silu(c_gate)], axis=2)
    proj = jnp.einsum('bsnw,nwd->bsnd', branches, w_branch)
    g = jax.nn.sigmoid(h @ merge_w + merge_b).reshape(b, s, N_BRANCH, D_MODEL)
    merged = jnp.sum(g * proj, axis=2)
    return x + gate * (merged @ w_out), k, v, finals


def setup_inputs(seed: int = 0) -> dict:
    key = jax.random.key(seed)
    ks = jax.random.split(key, 32)
    f32 = jnp.float32

    def nrm(k, shape, scale):
        return jax.random.normal(k, shape, f32) * scale

    a_im = (math.pi * jnp.arange(SSM_N, dtype=f32))[None, None, None, :] + nrm(ks[12], (DEPTH, N_DIR, SSM_G, SSM_N), 0.01)
    return {
        'x_prompt': nrm(ks[0], (BATCH, SEQ, D_MODEL), 1.0),
        'x_sample': nrm(ks[1], (DEC_BATCH, DEC_SEQ, D_MODEL), 1.0),
        'cache_k': nrm(ks[2], (DEC_BATCH, DEPTH, PAST_LEN, N_KV, HEAD_DIM), 1.0),
        'cache_v': nrm(ks[3], (DEC_BATCH, DEPTH, PAST_LEN, N_KV, HEAD_DIM), 1.0),
        'state_ssm': nrm(ks[4], (DEC_BATCH, DEPTH, N_DIR, SSM_G, SSM_N, 2), 0.1),
        'c': nrm(ks[5], (DEC_BATCH, D_MODEL), 1.0),
        'c_ctx': nrm(ks[6], (D_MODEL,), 1.0),
        'norm_g': 1.0 + nrm(ks[7], (DEPTH, D_MODEL), 0.02),
        'ada_w': nrm(ks[8], (DEPTH, D_MODEL, 3 * D_MODEL), D_MODEL ** -0.5),
        'ada_b': nrm(ks[9], (DEPTH, 3 * D_MODEL), 0.02),
        'w_in': nrm(ks[10], (DEPTH, D_MODEL, IN_WIDTH), D_MODEL ** -0.5),
        'ssm_a_re': -0.5 + nrm(ks[11], (DEPTH, N_DIR, SSM_G, SSM_N), 0.01),
        'ssm_a_im': a_im,
        'ssm_log_dt': jax.random.uniform(ks[13], (DEPTH, N_DIR, SSM_G), f32, math.log(DT_MIN), math.log(DT_MAX)),
        'ssm_b_re': nrm(ks[14], (DEPTH, N_DIR, SSM_G, SSM_N, SSM_P), (2 * SSM_P) ** -0.5),
        'ssm_b_im': nrm(ks[15], (DEPTH, N_DIR, SSM_G, SSM_N, SSM_P), (2 * SSM_P) ** -0.5),
        'ssm_c_re': nrm(ks[16], (DEPTH, N_DIR, SSM_G, SSM_P, SSM_N), SSM_N ** -0.5),
        'ssm_c_im': nrm(ks[17], (DEPTH, N_DIR, SSM_G, SSM_P, SSM_N), SSM_N ** -0.5),
        'ssm_d': nrm(ks[18], (DEPTH, BRANCH_W), 1.0),
        'glu_w': nrm(ks[19], (DEPTH, BRANCH_W, 2 * BRANCH_W), BRANCH_W ** -0.5),
        'fnet_w': nrm(ks[20], (DEPTH, BRANCH_W, BRANCH_W), BRANCH_W ** -0.5),
        'q_gain': 1.0 + nrm(ks[21], (DEPTH, HEAD_DIM), 0.02),
        'k_gain': 1.0 + nrm(ks[22], (DEPTH, HEAD_DIM), 0.02),
        'w_branch': nrm(ks[23], (DEPTH, N_BRANCH, BRANCH_W, D_MODEL), BRANCH_W ** -0.5),
        'merge_w': nrm(ks[24], (DEPTH, D_MODEL, N_BRANCH * D_MODEL), D_MODEL ** -0.5),
        'merge_b': nrm(ks[25], (DEPTH, N_BRANCH * D_MODEL), 0.02),
        'w_out': nrm(ks[26], (DEPTH, D_MODEL, D_MODEL), D_MODEL ** -0.5),
        'final_g': 1.0 + nrm(ks[27], (D_MODEL,), 0.02),
    }


def reference(x_prompt, x_sample, cache_k, cache_v, state_ssm, c, c_ctx,
              norm_g, ada_w, ada_b, w_in, ssm_a_re, ssm_a_im, ssm_log_dt,
              ssm_b_re, ssm_b_im, ssm_c_re, ssm_c_im, ssm_d, glu_w, fnet_w,
              q_gain, k_gain, w_branch, merge_w, merge_b, w_out, final_g):
    rope = axial_rope(x_sample.shape[1])
    xp, xs = x_prompt, x_sample
    new_k, new_v, new_s = [], [], []
    for l in range(DEPTH):
        lw = (norm_g[l], ada_w[l], ada_b[l], w_in[l], ssm_a_re[l], ssm_a_im[l], ssm_log_dt[l],
              ssm_b_re[l], ssm_b_im[l], ssm_c_re[l], ssm_c_im[l], ssm_d[l], glu_w[l], fnet_w[l],
              q_gain[l], k_gain[l], w_branch[l], merge_w[l], merge_b[l], w_out[l])
        xp, k_l, v_l, fin = mixer_layer(xp, c_ctx[None, :], lw, None, None, None)
        new_k.append(k_l)
        new_v.append(v_l)
        new_s.append(jnp.stack([jnp.stack([jnp.real(hf), jnp.imag(hf)], axis=-1) for hf in fin], axis=1))
        st = state_ssm[:, l].astype(jnp.float32)
        h0 = lax.complex(st[..., 0], st[..., 1])
        xs, _, _, _ = mixer_layer(xs, c, lw, (cache_k[:, l], cache_v[:, l]), (h0[:, 0], h0[:, 1]), rope)
    y_prompt = rmsnorm(xp, final_g)
    y_sample = rmsnorm(xs, final_g)
    new_cache_k = jnp.stack(new_k, axis=1)
    new_cache_v = jnp.stack(new_v, axis=1)
    new_state_ssm = jnp.stack(new_s, axis=1)
    return (y_prompt, y_sample, new_cache_k, new_cache_v, new_state_ssm)
```

```python
import math, contextlib
import numpy as np
import ml_dtypes
import concourse.bass as bass
import concourse.mybir as mybir
from concourse.bass_utils import run_bass_kernel_spmd

F32 = mybir.dt.float32
BF16 = mybir.dt.bfloat16
AF = mybir.ActivationFunctionType
ALU = mybir.AluOpType
PI = math.pi
NL = 4
EPS = 1e-6
O_AIN, O_AG, O_FIN, O_FG, O_Q, O_K, O_V, O_CG, W_INX = 0, 512, 1024, 1536, 2048, 2560, 2816, 2944, 3456
PERM = np.concatenate([np.arange(0, 64, 2), np.arange(1, 64, 2)])
DEBUG = False
MARKS = []
DBG_L = 0
DBG = {}


class _Rec:
    def __init__(self):
        self.call = None

    def __getattr__(self, name):
        def f(*args, **kw):
            self.call = (name, args, kw)
            return self
        return f


def _ap_name(x):
    try:
        return x.tensor.name
    except Exception:
        return None


class Chain:
    def __init__(self, sems):
        self.sems = sems
        self.cnt = {k: 0 for k in sems}
        self.prog = {k: [] for k in sems}
        self.waited = {k: {f: 0 for f in sems} for k in sems}
        self.res = {}
        self.fence = {k: 0 for k in sems}

    def barrier(self):
        self.fence = dict(self.cnt)

    def emit(self, name, fn, inc=1, queue=None):
        q = queue or name
        rec = _Rec()
        fn(rec)
        method, args, kw = rec.call
        reads, writes = [], []
        for i, a in enumerate(args):
            n = _ap_name(a)
            if n is not None:
                (writes if i == 0 else reads).append(n)
        for k, a in kw.items():
            n = _ap_name(a)
            if n is not None:
                (writes if k in ('out', 'accum_out') else reads).append(n)
        writes = writes + [n for n in reads if n.startswith('pb')]
        reads = [n for n in reads if not n.startswith('pb')]
        deps = dict(self.fence)
        def need(ec):
            if ec is not None and ec[1] > deps.get(ec[0], 0):
                deps[ec[0]] = ec[1]
        for n in reads:
            r = self.res.get(n)
            if r: need(r['w'])
        for n in writes:
            r = self.res.get(n)
            if r:
                need(r['w'])
                for e_, c_ in r['r'].items(): need((e_, c_))
        if name == 'dq':
            need(('dq', self.cnt['dq']))
        for f, c in deps.items():
            if c <= 0 or (f == name and name == 'pe'):
                continue
            if self.waited[q][f] >= c:
                continue
            self.waited[q][f] = c
            self.prog[q].append(lambda e, s=self.sems[f], v=c: e.wait_ge(s, v))
        self.cnt[name] += inc
        c = self.cnt[name]
        sem = self.sems[name]
        self.prog[q].append(lambda e, m=method, a=args, k=kw, sem=sem, inc=inc: getattr(e, m)(*a, **k).then_inc(sem, inc))
        for n in writes:
            self.res[n] = {'w': (name, c), 'r': {}}
        for n in reads:
            r = self.res.setdefault(n, {'w': None, 'r': {}})
            r['r'][name] = c


def build(n_layers=NL):
    nc = bass.Bass("TRN2", target_bir_lowering=False)

    def din(name, shape, dt=F32):
        return nc.dram_tensor(name, list(shape), dt, kind="ExternalInput").ap()

    def dout(name, shape):
        return nc.dram_tensor(name, list(shape), F32, kind="ExternalOutput").ap()

    def dscr(name, shape, dt=F32):
        return nc.dram_tensor(name, list(shape), dt, kind="Internal").ap()

    I = {}
    I['xp'] = din('xp', [1024, 512]); I['xs'] = din('xs', [1024, 2048])
    I['ck'] = din('ck', [NL, 2, 128, 256]); I['cv'] = din('cv', [NL, 256, 128])
    I['st'] = din('st', [NL, 64, 2, 64]); I['cc'] = din('cc', [128, 16])
    I['ng'] = din('ng', [128, 32]); I['fg'] = din('fg', [128, 8])
    I['adab'] = din('adab', [128, 96]); I['mb'] = din('mb', [128, 96])
    I['qk'] = din('qk', [128, 8]); I['dsk'] = din('dsk', [128, 128])
    I['ada_w'] = din('ada_w', [NL, 24, 128, 8, 128]); I['w_in'] = din('w_in', [NL, W_INX // 128, 128, 8, 128])
    I['glu_w'] = din('glu_w', [NL, 8, 128, 4, 128]); I['fnet_w'] = din('fnet_w', [NL, 4, 128, 4, 128])
    I['w_br'] = din('w_br', [NL * 3, 8, 128, 4, 128]); I['merge_w'] = din('merge_w', [NL, 24, 128, 8, 128])
    I['w_out'] = din('w_out', [NL * 1024, 1024])
    I['sa'] = din('sa', [NL, 64, 2, 64]); I['sdt'] = din('sdt', [NL, 64, 64])
    I['sb'] = din('sb', [NL, 64, 2, 64, 16]); I['sc'] = din('sc', [NL, 64, 2, 64, 16])
    I['cf'] = din('cf', [128, 6, 128])
    I['sel'] = din('sel', [128, 64, 128], BF16); I['mk'] = din('mk', [128, 2, 128])
    I['rc'] = din('rc', [128, 2048]); I['rs'] = din('rs', [128, 2048])
    I['dcc'] = din('dcc', [128, 2, 128], BF16)
    I['dpP'] = din('dpP', [2, 256, 256], BF16); I['dpS'] = din('dpS', [2, 2048, 2048], BF16)
    O = {}
    O['yp'] = dout('yp', [1024, 512]); O['ys'] = dout('ys', [1024, 2048])
    O['nk'] = dout('nk', [NL, 2, 64, 512]); O['nv'] = dout('nv', [NL, 512, 128])
    O['nst'] = dout('nst', [NL, 64, 2, 2, 32, 2])
    if DEBUG:
        O['dbgP'] = dout('dbgP', [3, 128, 4, 512]); O['dbgS'] = dout('dbgS', [3, 128, 4, 2048])
        O['dbgQ'] = dout('dbgQ', [128, 4, 2048]); O['dbgK'] = dout('dbgK', [128, 2, 2304]); O['dbgV'] = dout('dbgV', [128, 18, 2, 128])
    xd = dscr('xd', [1024, 2048])
    opM = dscr('opM', [NL, 128, 32, 2, 2, 64], BF16)
    opN = dscr('opN', [NL, 64, 32, 2, 2, 128], BF16)
    opT = dscr('opT', [NL, 128, 32, 128], BF16)
    opP = dscr('opP', [NL, 64, 2, 64, 16])
    opA = dscr('opA', [NL, 64, 2, 2, 64])

    with contextlib.ExitStack() as st:
        EC = st.enter_context

        uid = [0]
        CH = {}

        def sb(name, shape, dt=F32, stack=None):
            uid[0] += 1
            if 'ch' in CH:
                CH['ch'].barrier()
            return (stack or st).enter_context(nc.sbuf_tensor(f"t{uid[0]}_{name}", list(shape), dt))

        names = ['pe', 'act', 'dve', 'dq']
        sems = {k: EC(nc.semaphore(k)) for k in names}
        ch = Chain(sems)
        CH['ch'] = ch
        NSLOT = 4
        NSTG = 4
        wslots = [sb(f'wslot{i}', [128, 8, 128], BF16) for i in range(NSLOT)]
        wsems = [EC(nc.semaphore(f'ws{i}')) for i in range(NSLOT)]
        wstg = [sb(f'wstg{i}', [128, 8, 128], F32) for i in range(NSTG)]
        stsems = [EC(nc.semaphore(f'st{i}')) for i in range(NSTG)]
        wstate = {'n': 0, 'uses': [0] * NSLOT, 'last': [None] * NSLOT, 'hist': []}
        pool_prog = []
        sync_prog = []
        banks = [EC(nc.psum_tensor(f'pb{i}', [128, 512], F32)) for i in range(8)]
        bstate = {'i': 0}

        def mark(label): MARKS.append((label, dict(ch.cnt)))
        def V(fn): ch.emit('dve', fn)
        def A(fn): ch.emit('act', fn)
        def T(fn): ch.emit('pe', fn)
        def DM(out, in_): ch.emit('dq', lambda e: e.dma_start(out=out, in_=in_), inc=16, queue='act')

        def bank():
            bstate['i'] = (bstate['i'] + 1) % 6
            return banks[bstate['i']]

        def wissue(src, kc):
            k = wstate['n']; s = k % NSLOT; g = k % NSTG; wstate['n'] += 1
            if k >= NSTG:
                hs_, hv_ = wstate['hist'][k - NSTG]
                sync_prog.append(lambda e, hs_=hs_, hv_=hv_: e.wait_ge(hs_, hv_))
            sync_prog.append(lambda e, g=g, src=src, kc=kc: e.dma_start(out=wstg[g][:, 0:kc, :], in_=src).then_inc(stsems[g], 16))
            v = 16 * (k // NSTG + 1)
            if k % 3 == 0:
                wstate['uses'][s] += 1
                u = wstate['uses'][s]
                pool_prog.append(lambda e, g=g, v=v: e.wait_ge(stsems[g], v))
                lu = wstate['last'][s]
                if lu is not None:
                    pool_prog.append(lambda e, lu=lu: e.wait_ge(sems['pe'], lu))
                pool_prog.append(lambda e, s=s, g=g, kc=kc: e.tensor_copy(out=wslots[s][:, 0:kc, :], in_=wstg[g][:, 0:kc, :]).then_inc(wsems[s], 1))
                wstate['hist'].append((wsems[s], u))
                return (s, u)
            ch.prog['act'].append(lambda e, g=g, v=v: e.wait_ge(stsems[g], v))
            ch.emit('act', lambda e: e.activation(out=wslots[s][:, 0:kc, :], in_=wstg[g][:, 0:kc, :], func=AF.Identity))
            wstate['hist'].append((sems['act'], ch.cnt['act']))
            return (s, None)

        def wuse(h):
            s, u = h
            if u is not None:
                ch.prog['pe'].append(lambda e, s=s, u=u: e.wait_ge(wsems[s], u))
            return s

        def wload(src, kc):
            return wuse(wissue(src, kc))

        def wdone(s):
            wstate['last'][s] = ch.cnt['pe']

        def wcols(w, row0, K, c0):
            return w[row0:row0 + K, c0:c0 + 128].rearrange("(kc p) n -> p kc n", p=128)

        def proj(srcs, KC, rhs_fn, tiles, evac, n=None):
            hnd = [wissue(srcs[0], KC)]
            for j, src in enumerate(srcs):
                if j + 1 < len(srcs):
                    hnd.append(wissue(srcs[j + 1], KC))
                s = wuse(hnd[j])
                for t in tiles:
                    ps = bank()
                    nn = n if n is not None else t[1]
                    for kc in range(KC):
                        T(lambda e, ps=ps, s=s, kc=kc, t=t, nn=nn: e.matmul(ps[:, 0:nn], lhsT=wslots[s][:, kc, :], rhs=rhs_fn(kc, t), start=(kc == 0), stop=(kc == KC - 1)))
                    wdone(s)
                    evac(j, t, ps)

        cf = sb('cf', [128, 6, 128]); DM(cf[:], I['cf'])
        ident, swp, ob64, ob1024, ones = cf[:, 0, :], cf[:, 1, :], cf[:, 2, :], cf[:, 3, :], cf[:, 4, :]
        onesb = sb('onesb', [128, 128], BF16); V(lambda e: e.tensor_copy(out=onesb[:], in_=cf[:, 4, :]))
        epsT = sb('epsT', [128, 1]); V(lambda e: e.memset(epsT[:], EPS))
        smalls = {}
        for nm, w in [('cc', 16), ('ng', 32), ('fg', 8), ('adab', 96), ('mb', 96), ('qk', 8), ('dsk', 128)]:
            smalls[nm] = sb('s_' + nm, [128, w]); DM(smalls[nm][:], I[nm])
        dcc = sb('dcc', [128, 2, 128], BF16); DM(dcc[:], I['dcc'])
        mod = sb('mod', [128, 24]); gmod = sb('gmod', [128, 8]); scb = sb('scb', [128, 8, 1], BF16)
        scf = sb('scf', [128, 8])

        def cmul(o_r, o_i, a_r, a_i, b_r, b_i, t1, t2):
            V(lambda e: e.tensor_tensor(out=t1, in0=a_r, in1=b_r, op=ALU.mult))
            V(lambda e: e.tensor_tensor(out=t2, in0=a_i, in1=b_i, op=ALU.mult))
            V(lambda e: e.tensor_tensor(out=t1, in0=t1, in1=t2, op=ALU.subtract))
            V(lambda e: e.tensor_tensor(out=t2, in0=a_r, in1=b_i, op=ALU.mult))
            V(lambda e: e.tensor_tensor(out=o_i, in0=a_i, in1=b_r, op=ALU.mult))
            V(lambda e: e.tensor_tensor(out=o_i, in0=o_i, in1=t2, op=ALU.add))
            V(lambda e: e.tensor_copy(out=o_r, in_=t1))

        def cmul6(o_r, o_i, a_r, a_i, b_r, b_i, t1, t2, neg_i=False):
            V(lambda e: e.tensor_tensor(out=t1, in0=a_r, in1=b_r, op=ALU.mult))
            V(lambda e: e.tensor_tensor(out=t2, in0=a_i, in1=b_i, op=ALU.mult))
            V(lambda e: e.tensor_tensor(out=o_r, in0=t1, in1=t2, op=ALU.subtract))
            V(lambda e: e.tensor_tensor(out=t1, in0=a_r, in1=b_i, op=ALU.mult))
            V(lambda e: e.tensor_tensor(out=t2, in0=a_i, in1=b_r, op=ALU.mult))
            if neg_i:
                V(lambda e: e.scalar_tensor_tensor(out=o_i, in0=t1, scalar=-1.0, in1=t2, op0=ALU.mult, op1=ALU.subtract))
            else:
                V(lambda e: e.tensor_tensor(out=o_i, in0=t1, in1=t2, op=ALU.add))

        def s5_gen(l):
            mark(f'gen{l}')
            with contextlib.ExitStack() as g:
                def t(name, shape, dt=F32): return sb('g_' + name, shape, dt, g)
                sa = t('sa', [64, 2, 64]); ldt = t('ldt', [64, 64]); Bp = t('B', [64, 2, 64, 16]); Cp = t('C', [64, 2, 64, 16])
                DM(sa[:], I['sa'][l]); DM(ldt[:], I['sdt'][l]); DM(Bp[:], I['sb'][l]); DM(Cp[:], I['sc'][l])
                dt_ = t('dt', [64, 64]); rho = t('rho', [64, 64]); phi = t('phi', [64, 64])
                A(lambda e: e.activation(out=dt_[:], in_=ldt[:], func=AF.Exp))
                V(lambda e: e.tensor_tensor(out=rho[:], in0=sa[:, 0, :], in1=dt_[:], op=ALU.mult))
                V(lambda e: e.tensor_tensor(out=phi[:], in0=sa[:, 1, :], in1=dt_[:], op=ALU.mult))
                mag = t('mag', [64, 64]); magm = t('magm', [64, 64]); s1 = t('s1', [64, 64]); c1 = t('c1', [64, 64])
                r_ = t('r', [64, 64]); q_ = t('q', [64, 64]); t1 = t('t1', [64, 64]); t2 = t('t2', [64, 64])
                A(lambda e: e.activation(out=mag[:], in_=rho[:], func=AF.Exp))
                A(lambda e: e.activation(out=magm[:], in_=rho[:], func=AF.Exp, scale=-1.0))

                def rsin(out, shift):
                    V(lambda e: e.tensor_scalar(out=q_[:], in0=phi[:], scalar1=shift, scalar2=None, op0=ALU.add))
                    V(lambda e: e.tensor_copy(out=r_[:], in_=q_[:]))
                    for m in range(1, 7):
                        V(lambda e, m=m: e.tensor_scalar(out=t1[:], in0=q_[:], scalar1=(2 * m - 1) * PI, scalar2=-2 * PI, op0=ALU.is_gt, op1=ALU.mult))
                        V(lambda e: e.tensor_tensor(out=r_[:], in0=r_[:], in1=t1[:], op=ALU.add))
                    A(lambda e: e.activation(out=out, in_=r_[:], func=AF.Sin))
                rsin(s1[:], 0.0); rsin(c1[:], PI / 2)
                EP = t('EP', [64, 2, 16, 64])
                V(lambda e: e.memset(EP[:, 0, 7, :], 1.0)); V(lambda e: e.memset(EP[:, 1, 7, :], 0.0))
                V(lambda e: e.tensor_tensor(out=EP[:, 0, 8, :], in0=mag[:], in1=c1[:], op=ALU.mult))
                V(lambda e: e.tensor_tensor(out=EP[:, 1, 8, :], in0=mag[:], in1=s1[:], op=ALU.mult))
                V(lambda e: e.tensor_tensor(out=EP[:, 0, 6, :], in0=magm[:], in1=c1[:], op=ALU.mult))
                V(lambda e: e.scalar_tensor_tensor(out=EP[:, 1, 6, :], in0=magm[:], scalar=-1.0, in1=s1[:], op0=ALU.mult, op1=ALU.mult))
                for j in range(2, 9):
                    cmul(EP[:, 0, j + 7, :], EP[:, 1, j + 7, :], EP[:, 0, j + 6, :], EP[:, 1, j + 6, :], EP[:, 0, 8, :], EP[:, 1, 8, :], t1[:], t2[:])
                for j in range(2, 8):
                    cmul(EP[:, 0, 7 - j, :], EP[:, 1, 7 - j, :], EP[:, 0, 8 - j, :], EP[:, 1, 8 - j, :], EP[:, 0, 6, :], EP[:, 1, 6, :], t1[:], t2[:])
                er = t('er', [64, 64]); den = t('den', [64, 64]); wr = t('wr', [64, 64]); wi = t('wi', [64, 64])
                V(lambda e: e.tensor_scalar(out=er[:], in0=EP[:, 0, 8, :], scalar1=-1.0, scalar2=None, op0=ALU.add))
                V(lambda e: e.tensor_tensor(out=den[:], in0=sa[:, 0, :], in1=sa[:, 0, :], op=ALU.mult))
                V(lambda e: e.tensor_tensor(out=t1[:], in0=sa[:, 1, :], in1=sa[:, 1, :], op=ALU.mult))
                V(lambda e: e.tensor_tensor(out=den[:], in0=den[:], in1=t1[:], op=ALU.add))
                V(lambda e: e.reciprocal(out=den[:], in_=den[:]))
                V(lambda e: e.tensor_tensor(out=wr[:], in0=er[:], in1=sa[:, 0, :], op=ALU.mult))
                V(lambda e: e.tensor_tensor(out=t1[:], in0=EP[:, 1, 8, :], in1=sa[:, 1, :], op=ALU.mult))
                V(lambda e: e.tensor_tensor(out=wr[:], in0=wr[:], in1=t1[:], op=ALU.add))
                V(lambda e: e.tensor_tensor(out=wr[:], in0=wr[:], in1=den[:], op=ALU.mult))
                V(lambda e: e.tensor_tensor(out=wi[:], in0=EP[:, 1, 8, :], in1=sa[:, 0, :], op=ALU.mult))
                V(lambda e: e.tensor_tensor(out=t1[:], in0=er[:], in1=sa[:, 1, :], op=ALU.mult))
                V(lambda e: e.tensor_tensor(out=wi[:], in0=wi[:], in1=t1[:], op=ALU.subtract))
                V(lambda e: e.tensor_tensor(out=wi[:], in0=wi[:], in1=den[:], op=ALU.mult))
                Bb = t('Bb', [64, 2, 64, 16]); tb1 = t('tb1', [64, 64, 16]); tb2 = t('tb2', [64, 64, 16])
                wrb = wr[:].unsqueeze(2).broadcast_to([64, 64, 16]); wib = wi[:].unsqueeze(2).broadcast_to([64, 64, 16])
                cmul(Bb[:, 0], Bb[:, 1], wrb, wib, Bp[:, 0], Bp[:, 1], tb1[:], tb2[:])
                AA = t('AA', [64, 2, 2, 64]); PW = t('PW', [64, 2, 64, 16])
                V(lambda e: e.tensor_copy(out=AA[:, :, 0, :], in_=EP[:, :, 15, :]))
                V(lambda e: e.tensor_copy(out=PW[:, :, :, 0], in_=EP[:, :, 15, :]))
                for m in range(1, 16):
                    cmul(PW[:, 0, :, m], PW[:, 1, :, m], PW[:, 0, :, m - 1], PW[:, 1, :, m - 1], EP[:, 0, 15, :], EP[:, 1, 15, :], t1[:], t2[:])
                V(lambda e: e.tensor_copy(out=AA[:, :, 1, :], in_=PW[:, :, :, 15]))
                DM(opP[l], PW[:]); DM(opA[l], AA[:])
                TX = t('TX', [64, 3, 2, 8, 64])
                for sg in range(8):
                    for d in range(2):
                        ex = [(7 - sg, sg + 1, sg - 7), (sg, 8 - sg, -sg)][d]
                        for k in range(3):
                            V(lambda e, k=k, sg=sg, d=d, j=ex[k]: e.tensor_copy(out=TX[:, k, :, sg, d * 32:(d + 1) * 32], in_=EP[:, :, j + 7, d * 32:(d + 1) * 32]))
                Mt = t('Mt', [64, 2, 8, 8, 16]); Nt = t('Nt', [64, 2, 8, 8, 16]); Zt = t('Zt', [64, 2, 8, 8, 16])
                u1 = t('u1', [64, 8, 8, 16]); u2 = t('u2', [64, 8, 8, 16])
                Mst = t('Mst', [128, 8, 2, 64], BF16); Nst = t('Nst', [64, 8, 2, 128], BF16)
                Tacc = t('Tacc', [128, 8, 128]); Tb = t('Tb', [128, 8, 128], BF16); tt_ = t('tt', [128, 128])
                for gb in range(4):
                    for d in range(2):
                        c0 = d * 32 + gb * 8
                        def tx(k, ri): return TX[:, k, ri, :, c0:c0 + 8].rearrange("p s g -> p g s").unsqueeze(3).broadcast_to([64, 8, 8, 16])
                        def bb(P_, ri): return P_[:, ri, c0:c0 + 8, :].unsqueeze(2).broadcast_to([64, 8, 8, 16])
                        cmul6(Mt[:, 0], Mt[:, 1], tx(0, 0), tx(0, 1), bb(Bb, 0), bb(Bb, 1), u1[:], u2[:])
                        cmul6(Nt[:, 0], Nt[:, 1], tx(1, 0), tx(1, 1), bb(Cp, 0), bb(Cp, 1), u1[:], u2[:], neg_i=True)
                        cmul6(Zt[:, 0], Zt[:, 1], tx(2, 0), tx(2, 1), bb(Cp, 0), bb(Cp, 1), u1[:], u2[:], neg_i=True)
                        for gi in range(8):
                            ps = bank()
                            mr = Mt[:, 0, gi].rearrange("p s q -> p (s q)"); mi = Mt[:, 1, gi].rearrange("p s q -> p (s q)")
                            zr = Zt[:, 0, gi].rearrange("p s q -> p (s q)"); zi = Zt[:, 1, gi].rearrange("p s q -> p (s q)")
                            T(lambda e, ps=ps, mr=mr: e.matmul(ps[:, 0:64], lhsT=mr, rhs=cf[0:64, 0, 0:64], start=True, stop=True))
                            T(lambda e, ps=ps, mi=mi: e.matmul(ps[:, 64:128], lhsT=mi, rhs=cf[0:64, 0, 0:64], start=True, stop=True))
                            T(lambda e, ps=ps, mr=mr, zr=zr: e.matmul(ps[:, 128:256], lhsT=mr, rhs=zr, start=True, stop=False))
                            T(lambda e, ps=ps, mi=mi, zi=zi: e.matmul(ps[:, 128:256], lhsT=mi, rhs=zi, start=False, stop=True))
                            A(lambda e, ps=ps, gi=gi: e.activation(out=Mst[:, gi].rearrange("p r n -> p (r n)"), in_=ps[:, 0:128], func=AF.Identity))
                            if d == 0:
                                V(lambda e, ps=ps, gi=gi: e.tensor_tensor(out=Tacc[:, gi, :], in0=ps[:, 128:256], in1=mkT[:, 0, :], op=ALU.mult))
                            else:
                                V(lambda e, ps=ps: e.tensor_tensor(out=tt_[:], in0=ps[:, 128:256], in1=mkT[:, 1, :], op=ALU.mult))
                                V(lambda e, gi=gi: e.tensor_tensor(out=Tacc[:, gi, :], in0=Tacc[:, gi, :], in1=tt_[:], op=ALU.add))
                                V(lambda e, gi=gi, gg=gb * 8 + gi: e.scalar_tensor_tensor(out=Tb[:, gi, :], in0=ident, scalar=smalls['dsk'][:, l * 32 + gg:l * 32 + gg + 1], in1=Tacc[:, gi, :], op0=ALU.mult, op1=ALU.add))
                        V(lambda e: e.tensor_copy(out=Nst[:, :, 0, :], in_=Nt[:, 0].rearrange("p g s q -> p g (s q)")))
                        V(lambda e: e.tensor_copy(out=Nst[:, :, 1, :], in_=Nt[:, 1].rearrange("p g s q -> p g (s q)")))
                        DM(opM[l][:, gb * 8:(gb + 1) * 8, :, d], Mst[:])
                        DM(opN[l][:, gb * 8:(gb + 1) * 8, d], Nst[:])
                    DM(opT[l][:, gb * 8:(gb + 1) * 8, :], Tb[:])

        mkT = sb('mkT', [128, 2, 128]); DM(mkT[:], I['mk'])
        modall = sb('modall', [128, 2, NL, 24]); gmodall = sb('gmodall', [128, 2, NL, 8])
        scb2 = sb('scb2', [128, 8, 2], BF16); scf2 = sb('scf2', [128, 16])
        A(lambda e: e.activation(out=scf2[:], in_=smalls['cc'][:, 0:16], func=AF.Sigmoid))
        V(lambda e: e.tensor_tensor(out=scb2[:].rearrange("p k j -> p j k"), in0=scf2[:].rearrange("p (j k) -> p j k", j=2), in1=smalls['cc'][:, 0:16].rearrange("p (j k) -> p j k", j=2), op=ALU.mult))
        for l in range(n_layers):
            s5_gen(l)
            proj([I['ada_w'][l, j] for j in range(24)], 8, lambda kc, t_: scb2[:, kc, :], [(0, 2)],
                 lambda j, t_, ps, l=l: A(lambda e: e.activation(out=modall[:, :, l, j], in_=ps[:, 0:2], func=AF.Identity, bias=smalls['adab'][:, l * 24 + j:l * 24 + j + 1], scale=1.0)))
            for jb in range(2):
                V(lambda e: e.scalar_tensor_tensor(out=gmodall[:, jb, l, :], in0=modall[:, jb, l, 8:16], scalar=1.0, in1=smalls['ng'][:, l * 8:(l + 1) * 8], op0=ALU.add, op1=ALU.mult))

        def job(kind):
            S = (kind == 'S')
            TT = 2048 if S else 512
            NS, LS = (1, 2048) if S else (2, 256)
            CS = LS // 8; NB = CS // 16; C = TT // 8
            tiles = [(t0, 512) for t0 in range(0, TT, 512)]
            xin = I['xs'] if S else I['xp']
            yout = O['ys'] if S else O['yp']
            NKC = (LS + (256 if S else 0)) // 128
            with contextlib.ExitStack() as js:
                def t(name, shape, dt=F32, stack=None): return sb(kind + '_' + name, shape, dt, stack or js)
                hT = t('hT', [128, 8, TT], BF16)
                br = [t(f'br{n}', [128, 4, TT], BF16) for n in range(3)]
                tmpA = t('tmpA', [128, 512]); tmpB = t('tmpB', [128, 512])
                if not S:
                    dpP = t('dpP', [128, 2, 2, 256], BF16)
                    DM(dpP[:], I['dpP'].rearrange("t (c p) k -> p t c k", p=128))
                yst_t = None if S else t('yst', [128, 8, 512])
                X = {}
                X['stk'] = contextlib.ExitStack()
                X['xT'] = t('xT', [128, 8, TT], F32, X['stk'])
                DM(X['xT'][:], xin.rearrange("(kc p) t -> p kc t", p=128))

                def norm_tile(t0, gm_fn, sh_fn, out_fn, bf=True):
                    ps = banks[6]
                    xT = X['xT']
                    for kc in range(8):
                        V(lambda e, xa=xT[:, kc, t0:t0 + 512]: e.tensor_tensor(out=tmpA[:], in0=xa, in1=xa, op=ALU.mult))
                        T(lambda e, kc=kc: e.matmul(ps[:], lhsT=ob1024, rhs=tmpA[:], start=(kc == 0), stop=(kc == 7)))
                    A(lambda e: e.activation(out=tmpB[:], in_=ps[:], func=AF.Sqrt, bias=epsT[:], scale=1.0))
                    V(lambda e: e.reciprocal(out=tmpB[:], in_=tmpB[:]))
                    for kc in range(8):
                        V(lambda e, kc=kc, xa=xT[:, kc, t0:t0 + 512], gm=gm_fn(kc): e.scalar_tensor_tensor(out=tmpA[:], in0=xa, scalar=gm, in1=tmpB[:], op0=ALU.mult, op1=ALU.mult))
                        if sh_fn is None:
                            A(lambda e, o_=out_fn(kc): e.activation(out=o_, in_=tmpA[:], func=AF.Identity))
                        else:
                            A(lambda e, o_=out_fn(kc), b_=sh_fn(kc): e.activation(out=o_, in_=tmpA[:], func=AF.Identity, bias=b_, scale=1.0))

                def layer(l):
                    mark(f'{kind}{l}.ada')
                    mod = modall[:, 1 if S else 0, l, :]
                    gmod = gmodall[:, 1 if S else 0, l, :]
                    mark(f'{kind}{l}.norm')
                    for (t0, _) in tiles:
                        norm_tile(t0, lambda kc: gmod[:, kc:kc + 1], lambda kc: mod[:, kc:kc + 1], lambda kc, t0=t0: hT[:, kc, t0:t0 + 512])
                    DM(xd[:, 0:TT].rearrange("(kc p) t -> p kc t", p=128), X['xT'][:])
                    X['stk'].close()
                    hrhs = lambda kc, t_: hT[:, kc, t_[0]:t_[0] + t_[1]]
                    wi0 = l * 1024

                    mark(f'{kind}{l}.attnproj')
                    with contextlib.ExitStack() as a:
                        qT = t('qT', [128, 4, TT], BF16, a)
                        kT = t('kT', [128, 2, 2, NS, NKC * 128], BF16, a)
                        V(lambda e: e.memset(kT[:], 0.0))
                        vA = t('vA', [128, NS, NKC, 2, 128], BF16, a)
                        V(lambda e: e.memset(vA[:].rearrange("p s c k d -> p (s c k d)"), 1.0))
                        qf = t('qf', [128, 512], F32, a); kst = t('kst', [128, 512], F32, a)
                        ropes = contextlib.ExitStack()
                        if S:
                            rc = t('rc', [128, 2048], F32, ropes); rs_ = t('rs', [128, 2048], F32, ropes); DM(rc[:], I['rc']); DM(rs_[:], I['rs'])

                        def qk_evac(dst_fn, gcol, is_k):
                            def ev(j, t_, ps):
                                t0 = t_[0]
                                A(lambda e: e.activation(out=qf[:], in_=ps[:], func=AF.Identity))
                                V(lambda e: e.tensor_tensor(out=tmpA[:], in0=qf[:], in1=qf[:], op=ALU.mult))
                                p2 = banks[6]
                                T(lambda e: e.matmul(p2[:], lhsT=ob64, rhs=tmpA[:], start=True, stop=True))
                                A(lambda e: e.activation(out=tmpB[:], in_=p2[:], func=AF.Sqrt, bias=epsT[:], scale=1.0))
                                V(lambda e: e.reciprocal(out=tmpB[:], in_=tmpB[:]))
                                V(lambda e: e.scalar_tensor_tensor(out=qf[:], in0=qf[:], scalar=smalls['qk'][:, gcol:gcol + 1], in1=tmpB[:], op0=ALU.mult, op1=ALU.mult))
                                if is_k and not S:
                                    DM(O['nk'][l, j, :, t0:t0 + 512], qf[0:64, :])
                                if S:
                                    p3 = banks[7]
                                    T(lambda e: e.matmul(p3[:], lhsT=swp, rhs=qf[:], start=True, stop=True))
                                    V(lambda e: e.tensor_tensor(out=tmpA[:], in0=qf[:], in1=rc[:, t0:t0 + 512], op=ALU.mult))
                                    V(lambda e: e.tensor_tensor(out=tmpB[:], in0=p3[:], in1=rs_[:, t0:t0 + 512], op=ALU.mult))
                                    for (r_, d_) in dst_fn(j, t0):
                                        V(lambda e, r_=r_, d_=d_: e.tensor_tensor(out=d_, in0=tmpA[r_, :], in1=tmpB[r_, :], op=ALU.add))
                                else:
                                    for (r_, d_) in dst_fn(j, t0):
                                        V(lambda e, r_=r_, d_=d_: e.tensor_copy(out=d_, in_=qf[r_, :]))
                            return ev
                        proj([I['w_in'][l, O_Q // 128 + j] for j in range(4)], 8, hrhs, tiles,
                             qk_evac(lambda j, t0: [(slice(0, 128), qT[:, j, t0:t0 + 512])], l, False))
                        if S:
                            kdst = lambda j, t0: kT[:, j, 0, t0:t0 + 512]
                        else:
                            kdst = lambda j, t0: kT[:, j, :, 0:256]
                        proj([I['w_in'][l, O_K // 128 + j] for j in range(2)], 8, hrhs, tiles,
                             qk_evac((lambda j, t0: [(slice(h_ * 64, h_ * 64 + 64), kT[h_ * 64:h_ * 64 + 64, j, h_, 0, t0:t0 + 512]) for h_ in range(2)]) if S else (lambda j, t0: [(slice(h_ * 64, h_ * 64 + 64), kT[h_ * 64:h_ * 64 + 64, j, h_, :, :].rearrange("p s k -> p (s k)")) for h_ in range(2)]), 4 + l, True))
                        s = wload(I['w_in'][l, O_V // 128], 8)
                        for tk in range(TT // 128):
                            ps = bank()
                            for kc in range(8):
                                T(lambda e, ps=ps, kc=kc, tk=tk, s=s: e.matmul(ps[:, 0:128], lhsT=hT[:, kc, tk * 128:(tk + 1) * 128], rhs=wslots[s][:, kc, :], start=(kc == 0), stop=(kc == 7)))
                            wdone(s)
                            sq_, kc_ = (0, tk) if S else (tk // 2, tk % 2)
                            V(lambda e, ps=ps, sq_=sq_, kc_=kc_: e.tensor_copy(out=vA[:, sq_, kc_, :, 0:64], in_=ps[:, 0:128].rearrange("p (k d) -> p k d", k=2)))
                            if not S:
                                A(lambda e, ps=ps: e.activation(out=kst[:, 0:128], in_=ps[:, 0:128], func=AF.Identity))
                                DM(O['nv'][l, tk * 128:(tk + 1) * 128, :], kst[:, 0:128])
                        if S:
                            for kv in range(2):
                                DM(kst[:, 0:256], I['ck'][l, kv])
                                for h_ in range(2):
                                    V(lambda e, kv=kv, h_=h_: e.tensor_copy(out=kT[h_ * 64:h_ * 64 + 64, kv, h_, 0, 2048:2304], in_=kst[h_ * 64:h_ * 64 + 64, 0:256]))
                            for kc_ in range(2):
                                DM(kst[:, 0:128], I['cv'][l, kc_ * 128:(kc_ + 1) * 128, :])
                                V(lambda e, kc_=kc_: e.tensor_copy(out=vA[:, 0, 16 + kc_, :, 0:64], in_=kst[:, 0:128].rearrange("p (k d) -> p k d", k=2)))
                        proj([I['w_in'][l, O_CG // 128 + j] for j in range(4)], 8, hrhs, tiles,
                             lambda j, t_, ps: A(lambda e: e.activation(out=br[2][:, j, t_[0]:t_[0] + 512], in_=ps[:], func=AF.Silu)))
                        if DEBUG and S and l == DBG_L:
                            with contextlib.ExitStack() as a2:
                                dq = t('dq', [128, 2, 2304], F32, a2)
                                for hf in range(2):
                                    V(lambda e, hf=hf: e.tensor_copy(out=dq[:, :, 0:2048], in_=qT[:, 2 * hf:2 * hf + 2, :]))
                                    DM(O['dbgQ'][:, 2 * hf:2 * hf + 2, :], dq[:, :, 0:2048])
                                V(lambda e: e.tensor_copy(out=dq[:], in_=kT[:, :, 0, 0, :]))
                                DM(O['dbgK'], dq[:])
                                V(lambda e: e.tensor_copy(out=dq[:].rearrange("p a b -> p (a b)"), in_=vA[:, 0].rearrange("p c k d -> p (c k d)")))
                                DM(O['dbgV'].rearrange("p c k d -> p (c k d)"), dq[:].rearrange("p a b -> p (a b)"))
                        ropes.close()
                        mark(f'{kind}{l}.attncore')
                        exalls = [t(f'exall{i}', [128, NKC, 512], BF16, a) for i in range(2)]
                        hcount = [0]
                        for sq_ in range(NS):
                            for (q0, nq) in ([(t0, 512) for t0 in range(0, LS, 512)] if S else [(sq_ * 256, 256)]):
                                for hq in range(8):
                                    kv, qc, r0 = hq // 4, hq // 2, (hq % 2) * 64
                                    hp = hcount[0] % 2; hcount[0] += 1
                                    exall = exalls[hp]
                                    po, pz = banks[4 + 2 * hp], banks[5 + 2 * hp]
                                    for kc_ in range(NKC):
                                        ps = banks[kc_ % 4]
                                        T(lambda e, ps=ps, kc_=kc_, kv=kv, sq_=sq_, qc=qc, r0=r0, q0=q0, nq=nq: e.matmul(ps[:, 0:nq], lhsT=kT[:, kv, r0 // 64, sq_, kc_ * 128:(kc_ + 1) * 128], rhs=qT[:, qc, q0:q0 + nq], start=True, stop=True))
                                        A(lambda e, ps=ps, kc_=kc_, nq=nq: e.activation(out=exall[:, kc_, 0:nq], in_=ps[:, 0:nq], func=AF.Exp, scale=0.125))
                                    for kc_ in range(NKC):
                                        T(lambda e, kc_=kc_, kv=kv, sq_=sq_, nq=nq: e.matmul(po[:, 0:nq], lhsT=vA[:, sq_, kc_, kv, :], rhs=exall[:, kc_, 0:nq], start=(kc_ == 0), stop=(kc_ == NKC - 1)))
                                    A(lambda e: e.activation(out=tmpB[:, 0:nq], in_=po[:, 0:nq], func=AF.Identity))
                                    T(lambda e: e.matmul(pz[:, 0:nq], lhsT=cf[:, 5, :], rhs=tmpB[:, 0:nq], start=True, stop=True))
                                    rs0 = slice(r0, r0 + 64)
                                    den, num = (pz, tmpB) if r0 == 0 else (tmpB, pz)
                                    V(lambda e: e.reciprocal(out=tmpA[rs0, 0:nq], in_=den[rs0, 0:nq]))
                                    V(lambda e: e.tensor_tensor(out=tmpA[rs0, 0:nq], in0=num[rs0, 0:nq], in1=tmpA[rs0, 0:nq], op=ALU.mult))
                                    V(lambda e: e.tensor_tensor(out=br[2][rs0, qc, q0:q0 + nq], in0=tmpA[rs0, 0:nq], in1=br[2][rs0, qc, q0:q0 + nq], op=ALU.mult))

                    mark(f'{kind}{l}.fft')
                    with contextlib.ExitStack() as a:
                        fT = t('fT', [128, TT // 128, 512], BF16, a)
                        zT = t('zT', [128, 4, TT], BF16, a)
                        OW = 256
                        pq = t('pq', [128, 2, OW], BF16, a)
                        if S:
                            tabs = [t(f'tab{i}', [128, 2, 16, OW], BF16, a) for i in range(2)]

                            def tab_load(oc):
                                for tr in range(2):
                                    DM(tabs[oc % 2][:, tr], I['dpS'][tr, :, oc * OW:(oc + 1) * OW].rearrange("(c p) k -> p c k", p=128))
                            tab_load(0)
                        for j in range(4):
                            s = wload(I['w_in'][l, O_FIN // 128 + j], 8)
                            for tk in range(TT // 128):
                                ps = bank()
                                for kc in range(8):
                                    T(lambda e, ps=ps, kc=kc, tk=tk, s=s: e.matmul(ps[:, 0:128], lhsT=hT[:, kc, tk * 128:(tk + 1) * 128], rhs=wslots[s][:, kc, :], start=(kc == 0), stop=(kc == 7)))
                                wdone(s)
                                A(lambda e, ps=ps, tk=tk, j=j: e.activation(out=fT[:, tk, j * 128:(j + 1) * 128], in_=ps[:, 0:128], func=AF.Identity))
                        NIC = LS // 128
                        for sq_ in range(NS):
                            for oc in range(LS // OW):
                                if S:
                                    if oc + 1 < LS // OW:
                                        tab_load(oc + 1)
                                    tb = lambda tr, ic, tab=tabs[oc % 2]: tab[:, tr, ic, :]
                                else:
                                    tb = lambda tr, ic: dpP[:, tr, ic, :]
                                for g_ in range(4):
                                    pp = [bank(), bank()]
                                    for tr in range(2):
                                        for ic in range(NIC):
                                            T(lambda e, tr=tr, ic=ic, g_=g_, sq_=sq_, pp=pp, tb=tb: e.matmul(pp[tr][:, 0:OW], lhsT=fT[:, sq_ * NIC + ic, g_ * 128:(g_ + 1) * 128], rhs=tb(tr, ic), start=(ic == 0), stop=(ic == NIC - 1)))
                                    A(lambda e, pp=pp: e.activation(out=pq[:, 0, :], in_=pp[0][:, 0:OW], func=AF.Identity))
                                    V(lambda e, pp=pp: e.tensor_copy(out=pq[:, 1, :], in_=pp[1][:, 0:OW]))
                                    pz = bank()
                                    T(lambda e, pz=pz: e.matmul(pz[:, 0:OW], lhsT=dcc[:, 0, :], rhs=pq[:, 0, :], start=True, stop=False))
                                    T(lambda e, pz=pz: e.matmul(pz[:, 0:OW], lhsT=dcc[:, 1, :], rhs=pq[:, 1, :], start=False, stop=True))
                                    c0 = sq_ * LS + oc * OW
                                    A(lambda e, pz=pz, g_=g_, c0=c0: e.activation(out=zT[:, g_, c0:c0 + OW], in_=pz[:, 0:OW], func=AF.Identity))
                        proj([I['w_in'][l, O_FG // 128 + j] for j in range(4)], 8, hrhs, tiles,
                             lambda j, t_, ps: A(lambda e: e.activation(out=br[1][:, j, t_[0]:t_[0] + 512], in_=ps[:], func=AF.Silu)))
                        proj([I['fnet_w'][l, j] for j in range(4)], 4,
                             lambda kc, t_: zT[:, kc, t_[0]:t_[0] + t_[1]], tiles,
                             lambda j, t_, ps: V(lambda e: e.tensor_tensor(out=br[1][:, j, t_[0]:t_[0] + 512], in0=ps[:], in1=br[1][:, j, t_[0]:t_[0] + 512], op=ALU.mult)))

                    mark(f'{kind}{l}.s5front')
                    with contextlib.ExitStack() as a:
                        U8 = t('U8', [128, 32, C], BF16, a)
                        aT = br[0]
                        proj([I['w_in'][l, O_AIN // 128 + j] for j in range(4)], 8, hrhs, tiles,
                             lambda j, t_, ps: A(lambda e: e.activation(out=aT[:, j, t_[0]:t_[0] + 512], in_=ps[:], func=AF.Identity)))
                        with contextlib.ExitStack() as b0:
                            sel = t('sel', [128, 64, 128], BF16, b0); DM(sel[:], I['sel'])
                            for g_ in range(32):
                                ps = bank()
                                for s_ in range(8):
                                    T(lambda e, ps=ps, g_=g_, s_=s_, sel=sel: e.matmul(ps[:, 0:C], lhsT=sel[:, (g_ % 8) * 8 + s_, :], rhs=aT[:, g_ // 8, s_:TT:8], start=(s_ == 0), stop=(s_ == 7)))
                                A(lambda e, ps=ps, g_=g_: e.activation(out=U8[:, g_, :], in_=ps[:, 0:C], func=AF.Identity))

                        GP = 16 if S else 32

                        def s5_pass(gq):
                            with contextlib.ExitStack() as b:
                                g0 = gq * GP
                                Mw = t('Mw', [128, GP, 2, 2, 64], BF16, b); Nw = t('Nw', [128, GP, 2, 128], BF16, b); Tw = t('Tw', [128, GP, 128], BF16, b)
                                DM(Mw[:], opM[l][:, g0:g0 + GP]); DM(Tw[:], opT[l][:, g0:g0 + GP, :])
                                PWt = t('PWt', [128, 2, GP, 16], F32, b); AAt = t('AAt', [128, 2, 2, GP], F32, b)
                                for d in range(2):
                                    hs = slice(d * 64, d * 64 + 64)
                                    DM(Nw[hs], opN[l][:, g0:g0 + GP, d])
                                    DM(PWt[hs], opP[l][:, :, d * 32 + g0:d * 32 + g0 + GP, :])
                                    DM(AAt[hs], opA[l][:, :, :, d * 32 + g0:d * 32 + g0 + GP])
                                H = t('H', [128, 2, GP, NS, CS + 2], F32, b)
                                Hb = t('Hb', [128, 2, GP, NS, CS], BF16, b)
                                fin = t('fin', [128, 2, GP, NS], F32, b)
                                w1 = t('w1', [128, 2, GP, NB], F32, b); w2 = t('w2', [128, 2, GP, NB], F32, b)
                                s1 = w1[:].rearrange("p a g b -> p (a g b)")[:, 0:2 * NB * 16].rearrange("p (g b j) -> p g b j", g=2, b=NB)
                                s2 = w2[:].rearrange("p a g b -> p (a g b)")[:, 0:2 * NB * 16].rearrange("p (g b j) -> p g b j", g=2, b=NB)
                                V(lambda e: e.memset(H[:].rearrange("p r g s c -> p (r g s c)"), 0.0))
                                if S:
                                    stt_ = t('stt', [128, 2, GP], F32, b)
                                    for d in range(2):
                                        DM(stt_[d * 64:d * 64 + 64], I['st'][l][:, :, d * 32 + g0:d * 32 + g0 + GP])
                                    V(lambda e: e.tensor_copy(out=H[:, :, :, 0, 0], in_=stt_[:]))
                                for g_ in range(GP):
                                    for ri in range(2):
                                        ps = bank()
                                        T(lambda e: e.matmul(ps[:, 0:C], lhsT=Mw[:, g_, ri].rearrange("p d n -> p (d n)"), rhs=U8[:, g0 + g_, :], start=True, stop=True))
                                        V(lambda e: e.tensor_copy(out=H[0:64, ri, g_, :, 1:CS + 1], in_=ps[0:64, 0:C].rearrange("p (s c) -> p s c", s=NS)))
                                        A(lambda e: e.activation(out=H[64:128, ri, g_, :, CS:0:-1], in_=ps[64:128, 0:C].rearrange("p (s c) -> p s c", s=NS), func=AF.Identity))
                                AS = t('AS', [128, 2, 2, GP], F32, b)
                                V(lambda e: e.tensor_scalar(out=AS[:, 0], in0=AAt[:, 1], scalar1=-1.0, scalar2=None, op0=ALU.mult))
                                V(lambda e: e.tensor_copy(out=AS[:, 1], in_=AAt[:, 1]))
                                for sq_ in range(NS):
                                    Vall = H[:, :, :, sq_, 1:CS + 1].rearrange("p r g (b j) -> p r g b j", j=16)
                                    arb = AAt[:, 0, 0, :].unsqueeze(1).unsqueeze(3).broadcast_to([128, 2, GP, NB])
                                    asb = AS[:, :, 0, :].unsqueeze(3).broadcast_to([128, 2, GP, NB])
                                    for j in range(1, 16):
                                        cur = Vall[:, :, :, :, j]; prev = Vall[:, :, :, :, j - 1]; psw = Vall[:, ::-1, :, :, j - 1]
                                        V(lambda e: e.tensor_tensor(out=w1[:], in0=prev, in1=arb, op=ALU.mult))
                                        V(lambda e: e.tensor_tensor(out=w2[:], in0=psw, in1=asb, op=ALU.mult))
                                        V(lambda e: e.tensor_tensor(out=cur, in0=cur, in1=w1[:], op=ALU.add))
                                        V(lambda e: e.tensor_tensor(out=cur, in0=cur, in1=w2[:], op=ALU.add))
                                a16b = AAt[:, 0, 1, :].unsqueeze(1).unsqueeze(3).broadcast_to([128, 2, GP, NS])
                                as16 = AS[:, :, 1, :].unsqueeze(3).broadcast_to([128, 2, GP, NS])
                                for bq in range(NB):
                                    dst, src = 16 * (bq + 1), 16 * bq
                                    cur = H[:, :, :, :, dst]; prev = H[:, :, :, :, src]; psw = H[:, ::-1, :, :, src]
                                    V(lambda e: e.tensor_tensor(out=w1[:, :, :, 0:NS], in0=prev, in1=a16b, op=ALU.mult))
                                    V(lambda e: e.tensor_tensor(out=w2[:, :, :, 0:NS], in0=psw, in1=as16, op=ALU.mult))
                                    V(lambda e: e.tensor_tensor(out=cur, in0=cur, in1=w1[:, :, :, 0:NS], op=ALU.add))
                                    V(lambda e: e.tensor_tensor(out=cur, in0=cur, in1=w2[:, :, :, 0:NS], op=ALU.add))
                                for gb_ in range(0, GP, 2):
                                    for sq_ in range(NS):
                                        car_r = H[:, 0, gb_:gb_ + 2, sq_, 0:CS:16]; car_i = H[:, 1, gb_:gb_ + 2, sq_, 0:CS:16]
                                        cb = lambda x: x.unsqueeze(3).broadcast_to([128, 2, NB, 15])
                                        pb_ = lambda x: x.unsqueeze(2).broadcast_to([128, 2, NB, 15])
                                        pr = pb_(PWt[:, 0, gb_:gb_ + 2, 0:15]); pi_ = pb_(PWt[:, 1, gb_:gb_ + 2, 0:15])
                                        cr_, ci_ = cb(car_r), cb(car_i)
                                        u1 = s1[:, :, :, 0:15]; u2 = s2[:, :, :, 0:15]
                                        vr_ = H[:, 0, gb_:gb_ + 2, sq_, 1:CS + 1].rearrange("p g (b j) -> p g b j", j=16)[:, :, :, 0:15]
                                        vi_ = H[:, 1, gb_:gb_ + 2, sq_, 1:CS + 1].rearrange("p g (b j) -> p g b j", j=16)[:, :, :, 0:15]
                                        V(lambda e: e.tensor_tensor(out=u1, in0=cr_, in1=pr, op=ALU.mult))
                                        V(lambda e: e.tensor_tensor(out=u2, in0=ci_, in1=pi_, op=ALU.mult))
                                        V(lambda e: e.tensor_tensor(out=u1, in0=u1, in1=u2, op=ALU.subtract))
                                        V(lambda e: e.tensor_tensor(out=vr_, in0=vr_, in1=u1, op=ALU.add))
                                        V(lambda e: e.tensor_tensor(out=u1, in0=cr_, in1=pi_, op=ALU.mult))
                                        V(lambda e: e.tensor_tensor(out=u2, in0=ci_, in1=pr, op=ALU.mult))
                                        V(lambda e: e.tensor_tensor(out=u1, in0=u1, in1=u2, op=ALU.add))
                                        V(lambda e: e.tensor_tensor(out=vi_, in0=vi_, in1=u1, op=ALU.add))
                                for ri in range(2):
                                    V(lambda e: e.tensor_copy(out=Hb[0:64, ri], in_=H[0:64, ri, :, :, 0:CS]))
                                    V(lambda e: e.tensor_copy(out=Hb[64:128, ri], in_=H[64:128, ri, :, :, CS - 1::-1]))
                                if not S:
                                    V(lambda e: e.tensor_copy(out=fin[:], in_=H[:, :, :, :, CS]))
                                    for d in range(2):
                                        DM(O['nst'][l, :, :, d, g0:g0 + GP, :], fin[d * 64:d * 64 + 64])
                                for g_ in range(GP):
                                    gg = g0 + g_
                                    ps = bank()
                                    T(lambda e: e.matmul(ps[:, 0:C], lhsT=Tw[:, g_, :], rhs=U8[:, gg, :], start=True, stop=False))
                                    for ri in range(2):
                                        T(lambda e: e.matmul(ps[:, 0:C], lhsT=Nw[:, g_, ri, :], rhs=Hb[:, ri, g_].rearrange("p s c -> p (s c)"), start=False, stop=(ri == 1)))
                                    A(lambda e: e.activation(out=U8[:, gg, :], in_=ps[:, 0:C], func=AF.Identity))
                        mark(f'{kind}{l}.s5scan')
                        for gq in range(32 // GP):
                            s5_pass(gq)
                        mark(f'{kind}{l}.s5back')
                        with contextlib.ExitStack() as b0:
                            sel = t('sel', [128, 64, 128], BF16, b0); DM(sel[:], I['sel'])
                            for blk in range(4):
                                for half in range(2):
                                    pss = [bank(), bank()]
                                    for tq in range(4):
                                        tp = half * 4 + tq
                                        pv = pss[tq // 2][:, (tq % 2) * C:(tq % 2) * C + C]
                                        for gi in range(8):
                                            T(lambda e, pv=pv, tp=tp, gi=gi, blk=blk, sel=sel: e.matmul(pv, lhsT=sel[:, tp * 8 + gi, :], rhs=U8[:, blk * 8 + gi, :], start=(gi == 0), stop=(gi == 7)))
                                        A(lambda e, pv=pv, tp=tp, blk=blk: e.activation(out=aT[:, blk, tp:TT:8], in_=pv, func=AF.Identity))
                        for j in range(4):
                            for (t0, _) in tiles:
                                yv = aT[:, j, t0:t0 + 512]
                                V(lambda e, yv=yv: e.tensor_tensor(out=tmpA[:], in0=yv, in1=yv, op=ALU.mult))
                                V(lambda e: e.tensor_scalar(out=tmpA[:], in0=tmpA[:], scalar1=0.044715, scalar2=1.0, op0=ALU.mult, op1=ALU.add))
                                V(lambda e, yv=yv: e.tensor_tensor(out=tmpA[:], in0=tmpA[:], in1=yv, op=ALU.mult))
                                A(lambda e: e.activation(out=tmpB[:], in_=tmpA[:], func=AF.Sigmoid, scale=1.5957691216057308))
                                V(lambda e, yv=yv: e.tensor_tensor(out=yv, in0=tmpB[:], in1=yv, op=ALU.mult))
                        G4 = U8[:].rearrange("p g c -> p (g c)").rearrange("p (j t) -> p j t", j=4)
                        for j in range(4):
                            sv = wload(I['glu_w'][l, j], 4)
                            sg_ = wload(I['glu_w'][l, 4 + j], 4)
                            for (t0, _) in tiles:
                                pv_, pg_ = bank(), bank()
                                for kc in range(4):
                                    T(lambda e, kc=kc, pv_=pv_, t0=t0, sv=sv: e.matmul(pv_[:], lhsT=wslots[sv][:, kc, :], rhs=aT[:, kc, t0:t0 + 512], start=(kc == 0), stop=(kc == 3)))
                                for kc in range(4):
                                    T(lambda e, kc=kc, pg_=pg_, t0=t0, sg_=sg_: e.matmul(pg_[:], lhsT=wslots[sg_][:, kc, :], rhs=aT[:, kc, t0:t0 + 512], start=(kc == 0), stop=(kc == 3)))
                                wdone(sv); wdone(sg_)
                                A(lambda e, pg_=pg_: e.activation(out=tmpA[:], in_=pg_[:], func=AF.Sigmoid))
                                V(lambda e, pv_=pv_, j=j, t0=t0: e.tensor_tensor(out=G4[:, j, t0:t0 + 512], in0=pv_[:], in1=tmpA[:], op=ALU.mult))

                        def ag_evac(j, t_, ps):
                            A(lambda e: e.activation(out=tmpA[:], in_=ps[:], func=AF.Silu))
                            V(lambda e: e.tensor_tensor(out=br[0][:, j, t_[0]:t_[0] + 512], in0=tmpA[:], in1=G4[:, j, t_[0]:t_[0] + 512], op=ALU.mult))
                        proj([I['w_in'][l, O_AG // 128 + j] for j in range(4)], 8, hrhs, tiles, ag_evac)

                    if DEBUG and l == DBG_L:
                        with contextlib.ExitStack() as a:
                            dt_ = t('dbgt', [128, 4, TT], F32, a)
                            for n in range(3):
                                V(lambda e, n=n: e.tensor_copy(out=dt_[:], in_=br[n][:]))
                                DM(O['dbgS' if S else 'dbgP'][n], dt_[:])
                    mark(f'{kind}{l}.phaseC')
                    X['stk'] = contextlib.ExitStack()
                    X['xT'] = xT = t('xT', [128, 8, TT], F32, X['stk'])
                    DM(xT[:], xd[:, 0:TT].rearrange("(kc p) t -> p kc t", p=128))
                    with contextlib.ExitStack() as a:
                        mj = t('mj', [128, TT], F32, a); mjb = t('mjb', [128, TT], BF16, a)
                        for j in range(8):
                            for n in range(3):
                                sm = wload(I['merge_w'][l, n * 8 + j], 8)
                                sw_ = wload(I['w_br'][l * 3 + n, j], 4)
                                for (t0, _) in tiles:
                                    pg_, pp_ = bank(), bank()
                                    for kc in range(8):
                                        T(lambda e, kc=kc, pg_=pg_, t0=t0, sm=sm: e.matmul(pg_[:], lhsT=wslots[sm][:, kc, :], rhs=hT[:, kc, t0:t0 + 512], start=(kc == 0), stop=(kc == 7)))
                                    for kc in range(4):
                                        T(lambda e, kc=kc, pp_=pp_, t0=t0, sw_=sw_, n=n: e.matmul(pp_[:], lhsT=wslots[sw_][:, kc, :], rhs=br[n][:, kc, t0:t0 + 512], start=(kc == 0), stop=(kc == 3)))
                                    wdone(sm); wdone(sw_)
                                    bcol = l * 24 + n * 8 + j
                                    A(lambda e, pg_=pg_, bcol=bcol: e.activation(out=tmpA[:], in_=pg_[:], func=AF.Sigmoid, bias=smalls['mb'][:, bcol:bcol + 1], scale=1.0))
                                    if n == 0:
                                        V(lambda e, pp_=pp_, t0=t0: e.tensor_tensor(out=mj[:, t0:t0 + 512], in0=pp_[:], in1=tmpA[:], op=ALU.mult))
                                    else:
                                        V(lambda e, pp_=pp_: e.tensor_tensor(out=tmpA[:], in0=pp_[:], in1=tmpA[:], op=ALU.mult))
                                        V(lambda e, t0=t0: e.tensor_tensor(out=mj[:, t0:t0 + 512], in0=mj[:, t0:t0 + 512], in1=tmpA[:], op=ALU.add))
                            V(lambda e: e.tensor_copy(out=mjb[:], in_=mj[:]))
                            so = wload(I['w_out'][l * 1024 + j * 128:l * 1024 + (j + 1) * 128, :].rearrange("p (o n) -> p o n", n=128), 8)
                            for (t0, _) in tiles:
                                for o in range(8):
                                    ps = bank()
                                    T(lambda e, ps=ps, o=o, t0=t0, so=so: e.matmul(ps[:], lhsT=wslots[so][:, o, :], rhs=mjb[:, t0:t0 + 512], start=True, stop=True))
                                    V(lambda e, ps=ps, o=o, xa=xT[:, o, t0:t0 + 512]: e.scalar_tensor_tensor(out=xa, in0=ps[:], scalar=mod[:, 16 + o:17 + o], in1=xa, op0=ALU.mult, op1=ALU.add))
                            wdone(so)
                for l_ in range(n_layers):
                    layer(l_)
                mark(f'{kind}.final')
                if S:
                    yst = br[0][:].bitcast(F32).rearrange("p a b -> p (a b)").rearrange("p (k t) -> p k t", k=8)
                else:
                    yst = yst_t[:]
                for (t0, _) in tiles:
                    norm_tile(t0, lambda kc: smalls['fg'][:, kc:kc + 1], None, lambda kc: yst[:, kc, :])
                    DM(yout[:, t0:t0 + 512].rearrange("(kc p) t -> p kc t", p=128), yst)
                X['stk'].close()

        job('P')
        job('S')
        ch.barrier()
        ch.emit('dq', lambda e: e.dma_start(out=xd[0:1, 0:8], in_=xd[1:2, 0:8]), inc=16, queue='act')
        fin = ch.cnt['dq']
        with nc.Block() as block:
            @block.tensor
            def _(e):
                for f in ch.prog['pe']: f(e)
            @block.scalar
            def _(e):
                for f in ch.prog['act']: f(e)
                e.wait_ge(sems['dq'], fin)
            @block.vector
            def _(e):
                for f in ch.prog['dve']: f(e)
            @block.gpsimd
            def _(e):
                for f in pool_prog: f(e)
            @block.sync
            def _(e):
                for f in sync_prog: f(e)
    return nc


def _consts():
    bf = ml_dtypes.bfloat16
    c = {}
    cf = np.zeros((128, 6, 128), np.float32)
    cf[:, 0, :] = np.eye(128)
    sw = np.zeros((128, 128), np.float32)
    for h in range(2):
        for i in range(32):
            sw[h * 64 + 32 + i, h * 64 + i] = 1.0
            sw[h * 64 + i, h * 64 + 32 + i] = 1.0
    cf[:, 1, :] = sw
    ob = np.zeros((128, 128), np.float32); ob[0:64, 0:64] = 1 / 64.; ob[64:, 64:] = 1 / 64.
    cf[:, 2, :] = ob; cf[:, 3, :] = 1 / 1024.; cf[:, 4, :] = 1.0
    hsw = np.zeros((128, 128), np.float32); hsw[np.arange(64) + 64, np.arange(64)] = 1.0; hsw[np.arange(64), np.arange(64) + 64] = 1.0
    cf[:, 5, :] = hsw
    c['cf'] = cf
    sel = np.zeros((128, 64, 128), np.float32)
    for a in range(8):
        for b in range(8):
            for i in range(16):
                sel[16 * a + i, a * 8 + b, 16 * b + i] = 1.0
    c['sel'] = sel.astype(bf)
    mk = np.zeros((128, 2, 128), np.float32)
    for s in range(8):
        for t in range(8):
            if s <= t: mk[16 * s:16 * s + 16, 0, 16 * t:16 * t + 16] = 1.0
            if s >= t: mk[16 * s:16 * s + 16, 1, 16 * t:16 * t + 16] = 1.0
    c['mk'] = mk
    pos = np.arange(2048); row = (pos // 64).astype(np.float64); col = (pos % 64).astype(np.float64)
    inv = 10000.0 ** (-np.arange(16, dtype=np.float64) / 16)
    ang = np.concatenate([row[:, None] * inv, col[:, None] * inv], axis=-1).astype(np.float32)
    cos, sin = np.cos(ang).T, np.sin(ang).T
    rc = np.zeros((128, 2048), np.float32); rs = np.zeros((128, 2048), np.float32)
    for h in range(2):
        rc[h * 64:h * 64 + 32] = cos; rc[h * 64 + 32:h * 64 + 64] = cos
        rs[h * 64:h * 64 + 32] = -sin; rs[h * 64 + 32:h * 64 + 64] = sin
    c['rc'] = rc; c['rs'] = rs
    k = np.arange(128)
    a = 2 * np.pi * np.outer(k, k) / 128
    c['dcc'] = np.stack([np.cos(a), -np.sin(a)], axis=1).astype(np.float32).astype(bf)
    for nm, n in (('dpP', 256), ('dpS', 2048)):
        k = np.arange(n, dtype=np.int64)
        a = 2 * np.pi * (np.outer(k, k) % n).astype(np.float64) / n
        sc = 1.0 / math.sqrt(n * 128)
        c[nm] = np.stack([np.cos(a) * sc, np.sin(a) * sc]).astype(np.float32).astype(bf)
    return c


_NC = {}


def kernel(x_prompt, x_sample, cache_k, cache_v, state_ssm, c, c_ctx,
           norm_g, ada_w, ada_b, w_in, ssm_a_re, ssm_a_im, ssm_log_dt,
           ssm_b_re, ssm_b_im, ssm_c_re, ssm_c_im, ssm_d, glu_w, fnet_w,
           q_gain, k_gain, w_branch, merge_w, merge_b, w_out, final_g):
    f = lambda a: np.ascontiguousarray(np.asarray(a, dtype=np.float32))
    x_prompt, x_sample, cache_k, cache_v, state_ssm = map(f, (x_prompt, x_sample, cache_k, cache_v, state_ssm))
    vec = lambda v: f(np.asarray(v, np.float32).reshape(-1, 128).T)
    shared = _consts()
    shared['ng'] = f(np.concatenate([vec(norm_g[l]) for l in range(NL)], axis=1))
    shared['fg'] = vec(final_g)
    shared['adab'] = f(np.concatenate([vec(ada_b[l]) for l in range(NL)], axis=1))
    shared['mb'] = f(np.concatenate([vec(merge_b[l]) for l in range(NL)], axis=1))
    qg = np.asarray(q_gain, np.float32)[:, PERM]; kg = np.asarray(k_gain, np.float32)[:, PERM]
    shared['qk'] = f(np.concatenate([np.tile(qg.T, (2, 1)), np.tile(kg.T, (2, 1))], axis=1))
    sd = np.asarray(ssm_d, np.float32).reshape(NL, 32, 16)
    shared['dsk'] = f(np.tile(sd.transpose(2, 0, 1).reshape(16, NL * 32), (8, 1)))
    wi = np.asarray(w_in, np.float32)
    qcols = (2048 + (np.arange(8)[:, None] * 64 + PERM[None, :])).reshape(-1)
    k0 = 2560 + PERM; k1 = 2560 + 64 + PERM
    cols = np.concatenate([np.arange(0, 2048), qcols, k0, k0, k1, k1, np.arange(2688, 2816), np.arange(2816, 3328)])
    relay = lambda W, B, K, N: f(np.asarray(W, np.float32).reshape(B, K // 128, 128, N // 128, 128).transpose(0, 3, 2, 1, 4))
    shared['w_in'] = relay(wi[:, :, cols], NL, 1024, W_INX)
    shared['ada_w'] = relay(ada_w, NL, 1024, 3072)
    shared['glu_w'] = relay(glu_w, NL, 512, 1024)
    shared['fnet_w'] = relay(fnet_w, NL, 512, 512)
    shared['w_br'] = relay(np.asarray(w_branch, np.float32).reshape(NL * 3, 512, 1024), NL * 3, 512, 1024)
    shared['merge_w'] = relay(merge_w, NL, 1024, 3072)
    shared['w_out'] = f(np.asarray(w_out, np.float32).reshape(NL * 1024, 1024))
    are = np.asarray(ssm_a_re, np.float32).reshape(NL, 64, 64); aim = np.asarray(ssm_a_im, np.float32).reshape(NL, 64, 64)
    shared['sa'] = f(np.stack([are.transpose(0, 2, 1), aim.transpose(0, 2, 1)], axis=2))
    shared['sdt'] = f(np.broadcast_to(np.asarray(ssm_log_dt, np.float32).reshape(NL, 1, 64), (NL, 64, 64)))
    bre = np.asarray(ssm_b_re, np.float32).reshape(NL, 64, 64, 16); bim = np.asarray(ssm_b_im, np.float32).reshape(NL, 64, 64, 16)
    shared['sb'] = f(np.stack([bre.transpose(0, 2, 1, 3), bim.transpose(0, 2, 1, 3)], axis=2))
    cre = np.asarray(ssm_c_re, np.float32).reshape(NL, 64, 16, 64); cim = np.asarray(ssm_c_im, np.float32).reshape(NL, 64, 16, 64)
    shared['sc'] = f(np.stack([cre.transpose(0, 3, 1, 2), cim.transpose(0, 3, 1, 2)], axis=2))
    in_maps = []
    for core in range(8):
        m = dict(shared)
        sbi = core % 2
        m['xp'] = f(x_prompt[2 * core:2 * core + 2].reshape(512, 1024).T)
        m['xs'] = f(x_sample[sbi].T)
        ckp = cache_k[sbi][:, :, :, PERM]
        ckt = ckp.transpose(0, 2, 3, 1)
        m['ck'] = f(np.concatenate([ckt, ckt], axis=2))
        m['cv'] = f(cache_v[sbi].reshape(NL, 256, 128))
        stt = state_ssm[sbi]
        m['st'] = f(stt.transpose(0, 3, 4, 1, 2).reshape(NL, 64, 2, 64))
        m['cc'] = f(np.concatenate([vec(c_ctx), vec(np.asarray(c, np.float32)[sbi])], axis=1))
        in_maps.append(m)
    if 'nc' not in _NC:
        _NC['nc'] = build()
    res = run_bass_kernel_spmd(_NC['nc'], in_maps, core_ids=list(range(8)))
    R = res.results
    if DEBUG:
        DBG['P'] = [r['dbgP'] for r in R]; DBG['S'] = [r['dbgS'] for r in R]
        DBG['Q'] = R[0]['dbgQ']; DBG['K'] = R[0]['dbgK']; DBG['V'] = R[0]['dbgV']
    y_prompt = np.zeros((16, 256, 1024), np.float32); y_sample = np.zeros((2, 2048, 1024), np.float32)
    nk = np.zeros((16, NL, 256, 2, 64), np.float32); nv = np.zeros((16, NL, 256, 2, 64), np.float32)
    ns = np.zeros((16, NL, 2, 32, 64, 2), np.float32)
    inv = np.argsort(PERM)
    for core in range(8):
        r = R[core]
        y_prompt[2 * core:2 * core + 2] = r['yp'].T.reshape(2, 256, 1024)
        if core < 2:
            y_sample[core] = r['ys'].T
        k_ = r['nk'].reshape(NL, 2, 64, 2, 256)[:, :, inv]
        nk[2 * core:2 * core + 2] = k_.transpose(3, 0, 4, 1, 2)
        v_ = r['nv'].reshape(NL, 2, 256, 2, 64)
        nv[2 * core:2 * core + 2] = v_.transpose(1, 0, 2, 3, 4)
        s_ = r['nst']
        ns[2 * core:2 * core + 2] = s_.transpose(5, 0, 3, 4, 1, 2)
    return (y_prompt, y_sample, nk, nv, ns)
```

```python
import math, contextlib
import numpy as np
import ml_dtypes
import concourse.bass as bass
import concourse.mybir as mybir
from concourse.bass_utils import run_bass_kernel_spmd

F32 = mybir.dt.float32
BF16 = mybir.dt.bfloat16
AF = mybir.ActivationFunctionType
ALU = mybir.AluOpType
PI = math.pi
NL = 4
EPS = 1e-6
O_AIN, O_AG, O_FIN, O_FG, O_Q, O_K, O_V, O_CG, W_INX = 0, 512, 1024, 1536, 2048, 2560, 2816, 2944, 3456
PERM = np.concatenate([np.arange(0, 64, 2), np.arange(1, 64, 2)])
DEBUG = False
MARKS = []
DBG_L = 0
DBG = {}


class _Rec:
    def __init__(self):
        self.call = None

    def __getattr__(self, name):
        def f(*args, **kw):
            self.call = (name, args, kw)
            return self
        return f


def _ap_name(x):
    try:
        return x.tensor.name
    except Exception:
        return None


class Chain:
    def __init__(self, sems):
        self.sems = sems
        self.cnt = {k: 0 for k in sems}
        self.prog = {k: [] for k in sems}
        self.waited = {k: {f: 0 for f in sems} for k in sems}
        self.res = {}
        self.fence = {k: 0 for k in sems}

    def barrier(self):
        self.fence = dict(self.cnt)

    def emit(self, name, fn, inc=1, queue=None):
        q = queue or name
        rec = _Rec()
        fn(rec)
        method, args, kw = rec.call
        reads, writes = [], []
        for i, a in enumerate(args):
            n = _ap_name(a)
            if n is not None:
                (writes if i == 0 else reads).append(n)
        for k, a in kw.items():
            n = _ap_name(a)
            if n is not None:
                (writes if k in ('out', 'accum_out') else reads).append(n)
        writes = writes + [n for n in reads if n.startswith('pb')]
        reads = [n for n in reads if not n.startswith('pb')]
        deps = dict(self.fence)
        def need(ec):
            if ec is not None and ec[1] > deps.get(ec[0], 0):
                deps[ec[0]] = ec[1]
        for n in reads:
            r = self.res.get(n)
            if r: need(r['w'])
        for n in writes:
            r = self.res.get(n)
            if r:
                need(r['w'])
                for e_, c_ in r['r'].items(): need((e_, c_))
        if name == 'dq':
            need(('dq', self.cnt['dq']))
        for f, c in deps.items():
            if c <= 0 or (f == name and name == 'pe'):
                continue
            if self.waited[q][f] >= c:
                continue
            self.waited[q][f] = c
            self.prog[q].append(lambda e, s=self.sems[f], v=c: e.wait_ge(s, v))
        self.cnt[name] += inc
        c = self.cnt[name]
        sem = self.sems[name]
        self.prog[q].append(lambda e, m=method, a=args, k=kw, sem=sem, inc=inc: getattr(e, m)(*a, **k).then_inc(sem, inc))
        for n in writes:
            self.res[n] = {'w': (name, c), 'r': {}}
        for n in reads:
            r = self.res.setdefault(n, {'w': None, 'r': {}})
            r['r'][name] = c


def build(n_layers=NL):
    nc = bass.Bass("TRN2", target_bir_lowering=False)

    def din(name, shape, dt=F32):
        return nc.dram_tensor(name, list(shape), dt, kind="ExternalInput").ap()

    def dout(name, shape):
        return nc.dram_tensor(name, list(shape), F32, kind="ExternalOutput").ap()

    def dscr(name, shape, dt=F32):
        return nc.dram_tensor(name, list(shape), dt, kind="Internal").ap()

    I = {}
    I['xp'] = din('xp', [1024, 512]); I['xs'] = din('xs', [1024, 2048])
    I['ck'] = din('ck', [NL, 2, 128, 256]); I['cv'] = din('cv', [NL, 256, 128])
    I['st'] = din('st', [NL, 64, 2, 64]); I['cc'] = din('cc', [128, 16])
    I['ng'] = din('ng', [128, 32]); I['fg'] = din('fg', [128, 8])
    I['adab'] = din('adab', [128, 96]); I['mb'] = din('mb', [128, 96])
    I['qk'] = din('qk', [128, 8]); I['dsk'] = din('dsk', [128, 128])
    I['ada_w'] = din('ada_w', [NL, 24, 128, 8, 128]); I['w_in'] = din('w_in', [NL, W_INX // 128, 128, 8, 128])
    I['glu_w'] = din('glu_w', [NL, 8, 128, 4, 128]); I['fnet_w'] = din('fnet_w', [NL, 4, 128, 4, 128])
    I['w_br'] = din('w_br', [NL * 3, 8, 128, 4, 128]); I['merge_w'] = din('merge_w', [NL, 24, 128, 8, 128])
    I['w_out'] = din('w_out', [NL * 1024, 1024])
    I['sa'] = din('sa', [NL, 64, 2, 64]); I['sdt'] = din('sdt', [NL, 64, 64])
    I['sb'] = din('sb', [NL, 64, 2, 64, 16]); I['sc'] = din('sc', [NL, 64, 2, 64, 16])
    I['cf'] = din('cf', [128, 6, 128])
    I['sel'] = din('sel', [128, 64, 128], BF16); I['mk'] = din('mk', [128, 2, 128])
    I['rc'] = din('rc', [128, 2048]); I['rs'] = din('rs', [128, 2048])
    I['dcc'] = din('dcc', [128, 2, 128], BF16)
    I['dpP'] = din('dpP', [2, 256, 256], BF16); I['dpS'] = din('dpS', [2, 2048, 2048], BF16)
    O = {}
    O['yp'] = dout('yp', [1024, 512]); O['ys'] = dout('ys', [1024, 2048])
    O['nk'] = dout('nk', [NL, 2, 64, 512]); O['nv'] = dout('nv', [NL, 512, 128])
    O['nst'] = dout('nst', [NL, 64, 2, 2, 32, 2])
    if DEBUG:
        O['dbgP'] = dout('dbgP', [3, 128, 4, 512]); O['dbgS'] = dout('dbgS', [3, 128, 4, 2048])
        O['dbgQ'] = dout('dbgQ', [128, 4, 2048]); O['dbgK'] = dout('dbgK', [128, 2, 2304]); O['dbgV'] = dout('dbgV', [128, 18, 2, 128])
    xd = dscr('xd', [1024, 2048])
    opM = dscr('opM', [NL, 128, 32, 2, 2, 64], BF16)
    opN = dscr('opN', [NL, 64, 32, 2, 2, 128], BF16)
    opT = dscr('opT', [NL, 128, 32, 128], BF16)
    opP = dscr('opP', [NL, 64, 2, 64, 16])
    opA = dscr('opA', [NL, 64, 2, 2, 64])

    with contextlib.ExitStack() as st:
        EC = st.enter_context

        uid = [0]
        CH = {}

        def sb(name, shape, dt=F32, stack=None):
            uid[0] += 1
            if 'ch' in CH:
                CH['ch'].barrier()
            return (stack or st).enter_context(nc.sbuf_tensor(f"t{uid[0]}_{name}", list(shape), dt))

        names = ['pe', 'act', 'dve', 'dq']
        sems = {k: EC(nc.semaphore(k)) for k in names}
        ch = Chain(sems)
        CH['ch'] = ch
        NSLOT = 4
        NSTG = 4
        wslots = [sb(f'wslot{i}', [128, 8, 128], BF16) for i in range(NSLOT)]
        wsems = [EC(nc.semaphore(f'ws{i}')) for i in range(NSLOT)]
        wstg = [sb(f'wstg{i}', [128, 8, 128], F32) for i in range(NSTG)]
        stsems = [EC(nc.semaphore(f'st{i}')) for i in range(NSTG)]
        wstate = {'n': 0, 'uses': [0] * NSLOT, 'last': [None] * NSLOT, 'hist': []}
        pool_prog = []
        sync_prog = []
        banks = [EC(nc.psum_tensor(f'pb{i}', [128, 512], F32)) for i in range(8)]
        bstate = {'i': 0}

        def mark(label): MARKS.append((label, dict(ch.cnt)))
        def V(fn): ch.emit('dve', fn)
        def A(fn): ch.emit('act', fn)
        def T(fn): ch.emit('pe', fn)
        def DM(out, in_): ch.emit('dq', lambda e: e.dma_start(out=out, in_=in_), inc=16, queue='act')

        def bank():
            bstate['i'] = (bstate['i'] + 1) % 6
            return banks[bstate['i']]

        def wissue(src, kc):
            k = wstate['n']; s = k % NSLOT; g = k % NSTG; wstate['n'] += 1
            if k >= NSTG:
                hs_, hv_ = wstate['hist'][k - NSTG]
                sync_prog.append(lambda e, hs_=hs_, hv_=hv_: e.wait_ge(hs_, hv_))
            sync_prog.append(lambda e, g=g, src=src, kc=kc: e.dma_start(out=wstg[g][:, 0:kc, :], in_=src).then_inc(stsems[g], 16))
            v = 16 * (k // NSTG + 1)
            if k % 3 == 0:
                wstate['uses'][s] += 1
                u = wstate['uses'][s]
                pool_prog.append(lambda e, g=g, v=v: e.wait_ge(stsems[g], v))
                lu = wstate['last'][s]
                if lu is not None:
                    pool_prog.append(lambda e, lu=lu: e.wait_ge(sems['pe'], lu))
                pool_prog.append(lambda e, s=s, g=g, kc=kc: e.tensor_copy(out=wslots[s][:, 0:kc, :], in_=wstg[g][:, 0:kc, :]).then_inc(wsems[s], 1))
                wstate['hist'].append((wsems[s], u))
                return (s, u)
            ch.prog['act'].append(lambda e, g=g, v=v: e.wait_ge(stsems[g], v))
            ch.emit('act', lambda e: e.activation(out=wslots[s][:, 0:kc, :], in_=wstg[g][:, 0:kc, :], func=AF.Identity))
            wstate['hist'].append((sems['act'], ch.cnt['act']))
            return (s, None)

        def wuse(h):
            s, u = h
            if u is not None:
                ch.prog['pe'].append(lambda e, s=s, u=u: e.wait_ge(wsems[s], u))
            return s

        def wload(src, kc):
            return wuse(wissue(src, kc))

        def wdone(s):
            wstate['last'][s] = ch.cnt['pe']

        def wcols(w, row0, K, c0):
            return w[row0:row0 + K, c0:c0 + 128].rearrange("(kc p) n -> p kc n", p=128)

        def proj(srcs, KC, rhs_fn, tiles, evac, n=None):
            hnd = [wissue(srcs[0], KC)]
            for j, src in enumerate(srcs):
                if j + 1 < len(srcs):
                    hnd.append(wissue(srcs[j + 1], KC))
                s = wuse(hnd[j])
                for t in tiles:
                    ps = bank()
                    nn = n if n is not None else t[1]
                    for kc in range(KC):
                        T(lambda e, ps=ps, s=s, kc=kc, t=t, nn=nn: e.matmul(ps[:, 0:nn], lhsT=wslots[s][:, kc, :], rhs=rhs_fn(kc, t), start=(kc == 0), stop=(kc == KC - 1)))
                    wdone(s)
                    evac(j, t, ps)

        cf = sb('cf', [128, 6, 128]); DM(cf[:], I['cf'])
        ident, swp, ob64, ob1024, ones = cf[:, 0, :], cf[:, 1, :], cf[:, 2, :], cf[:, 3, :], cf[:, 4, :]
        onesb = sb('onesb', [128, 128], BF16); V(lambda e: e.tensor_copy(out=onesb[:], in_=cf[:, 4, :]))
        epsT = sb('epsT', [128, 1]); V(lambda e: e.memset(epsT[:], EPS))
        smalls = {}
        for nm, w in [('cc', 16), ('ng', 32), ('fg', 8), ('adab', 96), ('mb', 96), ('qk', 8), ('dsk', 128)]:
            smalls[nm] = sb('s_' + nm, [128, w]); DM(smalls[nm][:], I[nm])
        dcc = sb('dcc', [128, 2, 128], BF16); DM(dcc[:], I['dcc'])
        mod = sb('mod', [128, 24]); gmod = sb('gmod', [128, 8]); scb = sb('scb', [128, 8, 1], BF16)
        scf = sb('scf', [128, 8])

        def cmul(o_r, o_i, a_r, a_i, b_r, b_i, t1, t2):
            V(lambda e: e.tensor_tensor(out=t1, in0=a_r, in1=b_r, op=ALU.mult))
            V(lambda e: e.tensor_tensor(out=t2, in0=a_i, in1=b_i, op=ALU.mult))
            V(lambda e: e.tensor_tensor(out=t1, in0=t1, in1=t2, op=ALU.subtract))
            V(lambda e: e.tensor_tensor(out=t2, in0=a_r, in1=b_i, op=ALU.mult))
            V(lambda e: e.tensor_tensor(out=o_i, in0=a_i, in1=b_r, op=ALU.mult))
            V(lambda e: e.tensor_tensor(out=o_i, in0=o_i, in1=t2, op=ALU.add))
            V(lambda e: e.tensor_copy(out=o_r, in_=t1))

        def cmul6(o_r, o_i, a_r, a_i, b_r, b_i, t1, t2, neg_i=False):
            V(lambda e: e.tensor_tensor(out=t1, in0=a_r, in1=b_r, op=ALU.mult))
            V(lambda e: e.tensor_tensor(out=t2, in0=a_i, in1=b_i, op=ALU.mult))
            V(lambda e: e.tensor_tensor(out=o_r, in0=t1, in1=t2, op=ALU.subtract))
            V(lambda e: e.tensor_tensor(out=t1, in0=a_r, in1=b_i, op=ALU.mult))
            V(lambda e: e.tensor_tensor(out=t2, in0=a_i, in1=b_r, op=ALU.mult))
            if neg_i:
                V(lambda e: e.scalar_tensor_tensor(out=o_i, in0=t1, scalar=-1.0, in1=t2, op0=ALU.mult, op1=ALU.subtract))
            else:
                V(lambda e: e.tensor_tensor(out=o_i, in0=t1, in1=t2, op=ALU.add))

        def s5_gen(l):
            mark(f'gen{l}')
            with contextlib.ExitStack() as g:
                def t(name, shape, dt=F32): return sb('g_' + name, shape, dt, g)
                sa = t('sa', [64, 2, 64]); ldt = t('ldt', [64, 64]); Bp = t('B', [64, 2, 64, 16]); Cp = t('C', [64, 2, 64, 16])
                DM(sa[:], I['sa'][l]); DM(ldt[:], I['sdt'][l]); DM(Bp[:], I['sb'][l]); DM(Cp[:], I['sc'][l])
                dt_ = t('dt', [64, 64]); rho = t('rho', [64, 64]); phi = t('phi', [64, 64])
                A(lambda e: e.activation(out=dt_[:], in_=ldt[:], func=AF.Exp))
                V(lambda e: e.tensor_tensor(out=rho[:], in0=sa[:, 0, :], in1=dt_[:], op=ALU.mult))
                V(lambda e: e.tensor_tensor(out=phi[:], in0=sa[:, 1, :], in1=dt_[:], op=ALU.mult))
                mag = t('mag', [64, 64]); magm = t('magm', [64, 64]); s1 = t('s1', [64, 64]); c1 = t('c1', [64, 64])
                r_ = t('r', [64, 64]); q_ = t('q', [64, 64]); t1 = t('t1', [64, 64]); t2 = t('t2', [64, 64])
                A(lambda e: e.activation(out=mag[:], in_=rho[:], func=AF.Exp))
                A(lambda e: e.activation(out=magm[:], in_=rho[:], func=AF.Exp, scale=-1.0))

                def rsin(out, shift):
                    V(lambda e: e.tensor_scalar(out=q_[:], in0=phi[:], scalar1=shift, scalar2=None, op0=ALU.add))
                    V(lambda e: e.tensor_copy(out=r_[:], in_=q_[:]))
                    for m in range(1, 7):
                        V(lambda e, m=m: e.tensor_scalar(out=t1[:], in0=q_[:], scalar1=(2 * m - 1) * PI, scalar2=-2 * PI, op0=ALU.is_gt, op1=ALU.mult))
                        V(lambda e: e.tensor_tensor(out=r_[:], in0=r_[:], in1=t1[:], op=ALU.add))
                    A(lambda e: e.activation(out=out, in_=r_[:], func=AF.Sin))
                rsin(s1[:], 0.0); rsin(c1[:], PI / 2)
                EP = t('EP', [64, 2, 16, 64])
                V(lambda e: e.memset(EP[:, 0, 7, :], 1.0)); V(lambda e: e.memset(EP[:, 1, 7, :], 0.0))
                V(lambda e: e.tensor_tensor(out=EP[:, 0, 8, :], in0=mag[:], in1=c1[:], op=ALU.mult))
                V(lambda e: e.tensor_tensor(out=EP[:, 1, 8, :], in0=mag[:], in1=s1[:], op=ALU.mult))
                V(lambda e: e.tensor_tensor(out=EP[:, 0, 6, :], in0=magm[:], in1=c1[:], op=ALU.mult))
                V(lambda e: e.scalar_tensor_tensor(out=EP[:, 1, 6, :], in0=magm[:], scalar=-1.0, in1=s1[:], op0=ALU.mult, op1=ALU.mult))
                for j in range(2, 9):
                    cmul6(EP[:, 0, j + 7, :], EP[:, 1, j + 7, :], EP[:, 0, j + 6, :], EP[:, 1, j + 6, :], EP[:, 0, 8, :], EP[:, 1, 8, :], t1[:], t2[:])
                for j in range(2, 8):
                    cmul6(EP[:, 0, 7 - j, :], EP[:, 1, 7 - j, :], EP[:, 0, 8 - j, :], EP[:, 1, 8 - j, :], EP[:, 0, 6, :], EP[:, 1, 6, :], t1[:], t2[:])
                er = t('er', [64, 64]); den = t('den', [64, 64]); wr = t('wr', [64, 64]); wi = t('wi', [64, 64])
                V(lambda e: e.tensor_scalar(out=er[:], in0=EP[:, 0, 8, :], scalar1=-1.0, scalar2=None, op0=ALU.add))
                V(lambda e: e.tensor_tensor(out=den[:], in0=sa[:, 0, :], in1=sa[:, 0, :], op=ALU.mult))
                V(lambda e: e.tensor_tensor(out=t1[:], in0=sa[:, 1, :], in1=sa[:, 1, :], op=ALU.mult))
                V(lambda e: e.tensor_tensor(out=den[:], in0=den[:], in1=t1[:], op=ALU.add))
                V(lambda e: e.reciprocal(out=den[:], in_=den[:]))
                V(lambda e: e.tensor_tensor(out=wr[:], in0=er[:], in1=sa[:, 0, :], op=ALU.mult))
                V(lambda e: e.tensor_tensor(out=t1[:], in0=EP[:, 1, 8, :], in1=sa[:, 1, :], op=ALU.mult))
                V(lambda e: e.tensor_tensor(out=wr[:], in0=wr[:], in1=t1[:], op=ALU.add))
                V(lambda e: e.tensor_tensor(out=wr[:], in0=wr[:], in1=den[:], op=ALU.mult))
                V(lambda e: e.tensor_tensor(out=wi[:], in0=EP[:, 1, 8, :], in1=sa[:, 0, :], op=ALU.mult))
                V(lambda e: e.tensor_tensor(out=t1[:], in0=er[:], in1=sa[:, 1, :], op=ALU.mult))
                V(lambda e: e.tensor_tensor(out=wi[:], in0=wi[:], in1=t1[:], op=ALU.subtract))
                V(lambda e: e.tensor_tensor(out=wi[:], in0=wi[:], in1=den[:], op=ALU.mult))
                Bb = t('Bb', [64, 2, 64, 16]); tb1 = t('tb1', [64, 64, 16]); tb2 = t('tb2', [64, 64, 16])
                wrb = wr[:].unsqueeze(2).broadcast_to([64, 64, 16]); wib = wi[:].unsqueeze(2).broadcast_to([64, 64, 16])
                cmul6(Bb[:, 0], Bb[:, 1], wrb, wib, Bp[:, 0], Bp[:, 1], tb1[:], tb2[:])
                AA = t('AA', [64, 2, 2, 64]); PW = t('PW', [64, 2, 64, 16])
                V(lambda e: e.tensor_copy(out=AA[:, :, 0, :], in_=EP[:, :, 15, :]))
                V(lambda e: e.tensor_copy(out=PW[:, :, :, 0], in_=EP[:, :, 15, :]))
                for m in range(1, 16):
                    cmul6(PW[:, 0, :, m], PW[:, 1, :, m], PW[:, 0, :, m - 1], PW[:, 1, :, m - 1], EP[:, 0, 15, :], EP[:, 1, 15, :], t1[:], t2[:])
                V(lambda e: e.tensor_copy(out=AA[:, :, 1, :], in_=PW[:, :, :, 15]))
                DM(opP[l], PW[:]); DM(opA[l], AA[:])
                TX = t('TX', [64, 3, 2, 8, 64])
                for sg in range(8):
                    for d in range(2):
                        ex = [(7 - sg, sg + 1, sg - 7), (sg, 8 - sg, -sg)][d]
                        for k in range(3):
                            V(lambda e, k=k, sg=sg, d=d, j=ex[k]: e.tensor_copy(out=TX[:, k, :, sg, d * 32:(d + 1) * 32], in_=EP[:, :, j + 7, d * 32:(d + 1) * 32]))
                Mt = t('Mt', [64, 2, 8, 8, 16]); Nt = t('Nt', [64, 2, 8, 8, 16]); Zt = t('Zt', [64, 2, 8, 8, 16])
                u1 = t('u1', [64, 8, 8, 16]); u2 = t('u2', [64, 8, 8, 16])
                Mst = t('Mst', [128, 8, 2, 64], BF16); Nst = t('Nst', [64, 8, 2, 128], BF16)
                Tacc = t('Tacc', [128, 8, 128]); Tb = t('Tb', [128, 8, 128], BF16); tt_ = t('tt', [128, 128])
                for gb in range(4):
                    for d in range(2):
                        c0 = d * 32 + gb * 8
                        def tx(k, ri): return TX[:, k, ri, :, c0:c0 + 8].rearrange("p s g -> p g s").unsqueeze(3).broadcast_to([64, 8, 8, 16])
                        def bb(P_, ri): return P_[:, ri, c0:c0 + 8, :].unsqueeze(2).broadcast_to([64, 8, 8, 16])
                        cmul6(Mt[:, 0], Mt[:, 1], tx(0, 0), tx(0, 1), bb(Bb, 0), bb(Bb, 1), u1[:], u2[:])
                        cmul6(Nt[:, 0], Nt[:, 1], tx(1, 0), tx(1, 1), bb(Cp, 0), bb(Cp, 1), u1[:], u2[:], neg_i=True)
                        cmul6(Zt[:, 0], Zt[:, 1], tx(2, 0), tx(2, 1), bb(Cp, 0), bb(Cp, 1), u1[:], u2[:], neg_i=True)
                        for gi in range(8):
                            ps = bank()
                            mr = Mt[:, 0, gi].rearrange("p s q -> p (s q)"); mi = Mt[:, 1, gi].rearrange("p s q -> p (s q)")
                            zr = Zt[:, 0, gi].rearrange("p s q -> p (s q)"); zi = Zt[:, 1, gi].rearrange("p s q -> p (s q)")
                            T(lambda e, ps=ps, mr=mr: e.matmul(ps[:, 0:64], lhsT=mr, rhs=cf[0:64, 0, 0:64], start=True, stop=True))
                            T(lambda e, ps=ps, mi=mi: e.matmul(ps[:, 64:128], lhsT=mi, rhs=cf[0:64, 0, 0:64], start=True, stop=True))
                            T(lambda e, ps=ps, mr=mr, zr=zr: e.matmul(ps[:, 128:256], lhsT=mr, rhs=zr, start=True, stop=False))
                            T(lambda e, ps=ps, mi=mi, zi=zi: e.matmul(ps[:, 128:256], lhsT=mi, rhs=zi, start=False, stop=True))
                            A(lambda e, ps=ps, gi=gi: e.activation(out=Mst[:, gi].rearrange("p r n -> p (r n)"), in_=ps[:, 0:128], func=AF.Identity))
                            if d == 0:
                                V(lambda e, ps=ps, gi=gi: e.tensor_tensor(out=Tacc[:, gi, :], in0=ps[:, 128:256], in1=mkT[:, 0, :], op=ALU.mult))
                            else:
                                V(lambda e, ps=ps: e.tensor_tensor(out=tt_[:], in0=ps[:, 128:256], in1=mkT[:, 1, :], op=ALU.mult))
                                V(lambda e, gi=gi: e.tensor_tensor(out=Tacc[:, gi, :], in0=Tacc[:, gi, :], in1=tt_[:], op=ALU.add))
                                V(lambda e, gi=gi, gg=gb * 8 + gi: e.scalar_tensor_tensor(out=Tb[:, gi, :], in0=ident, scalar=smalls['dsk'][:, l * 32 + gg:l * 32 + gg + 1], in1=Tacc[:, gi, :], op0=ALU.mult, op1=ALU.add))
                        V(lambda e: e.tensor_copy(out=Nst[:, :, 0, :], in_=Nt[:, 0].rearrange("p g s q -> p g (s q)")))
                        V(lambda e: e.tensor_copy(out=Nst[:, :, 1, :], in_=Nt[:, 1].rearrange("p g s q -> p g (s q)")))
                        DM(opM[l][:, gb * 8:(gb + 1) * 8, :, d], Mst[:])
                        DM(opN[l][:, gb * 8:(gb + 1) * 8, d], Nst[:])
                    DM(opT[l][:, gb * 8:(gb + 1) * 8, :], Tb[:])

        mkT = sb('mkT', [128, 2, 128]); DM(mkT[:], I['mk'])
        modall = sb('modall', [128, 2, NL, 24]); gmodall = sb('gmodall', [128, 2, NL, 8])
        scb2 = sb('scb2', [128, 8, 2], BF16); scf2 = sb('scf2', [128, 16])
        A(lambda e: e.activation(out=scf2[:], in_=smalls['cc'][:, 0:16], func=AF.Sigmoid))
        V(lambda e: e.tensor_tensor(out=scb2[:].rearrange("p k j -> p j k"), in0=scf2[:].rearrange("p (j k) -> p j k", j=2), in1=smalls['cc'][:, 0:16].rearrange("p (j k) -> p j k", j=2), op=ALU.mult))
        for l in range(n_layers):
            s5_gen(l)
            proj([I['ada_w'][l, j] for j in range(24)], 8, lambda kc, t_: scb2[:, kc, :], [(0, 2)],
                 lambda j, t_, ps, l=l: A(lambda e: e.activation(out=modall[:, :, l, j], in_=ps[:, 0:2], func=AF.Identity, bias=smalls['adab'][:, l * 24 + j:l * 24 + j + 1], scale=1.0)))
            for jb in range(2):
                V(lambda e: e.scalar_tensor_tensor(out=gmodall[:, jb, l, :], in0=modall[:, jb, l, 8:16], scalar=1.0, in1=smalls['ng'][:, l * 8:(l + 1) * 8], op0=ALU.add, op1=ALU.mult))

        def job(kind):
            S = (kind == 'S')
            TT = 2048 if S else 512
            NS, LS = (1, 2048) if S else (2, 256)
            CS = LS // 8; NB = CS // 16; C = TT // 8
            tiles = [(t0, 512) for t0 in range(0, TT, 512)]
            xin = I['xs'] if S else I['xp']
            yout = O['ys'] if S else O['yp']
            NKC = (LS + (256 if S else 0)) // 128
            with contextlib.ExitStack() as js:
                def t(name, shape, dt=F32, stack=None): return sb(kind + '_' + name, shape, dt, stack or js)
                hT = t('hT', [128, 8, TT], BF16)
                br = [t(f'br{n}', [128, 4, TT], BF16) for n in range(3)]
                tmpA = t('tmpA', [128, 512]); tmpB = t('tmpB', [128, 512])
                if not S:
                    dpP = t('dpP', [128, 2, 2, 256], BF16)
                    DM(dpP[:], I['dpP'].rearrange("t (c p) k -> p t c k", p=128))
                yst_t = None if S else t('yst', [128, 8, 512])
                X = {}
                X['stk'] = contextlib.ExitStack()
                X['xT'] = t('xT', [128, 8, TT], F32, X['stk'])
                DM(X['xT'][:], xin.rearrange("(kc p) t -> p kc t", p=128))

                def norm_tile(t0, gm_fn, sh_fn, out_fn, bf=True):
                    ps = banks[6]
                    xT = X['xT']
                    for kc in range(8):
                        V(lambda e, xa=xT[:, kc, t0:t0 + 512]: e.tensor_tensor(out=tmpA[:], in0=xa, in1=xa, op=ALU.mult))
                        T(lambda e, kc=kc: e.matmul(ps[:], lhsT=ob1024, rhs=tmpA[:], start=(kc == 0), stop=(kc == 7)))
                    A(lambda e: e.activation(out=tmpB[:], in_=ps[:], func=AF.Sqrt, bias=epsT[:], scale=1.0))
                    V(lambda e: e.reciprocal(out=tmpB[:], in_=tmpB[:]))
                    for kc in range(8):
                        V(lambda e, kc=kc, xa=xT[:, kc, t0:t0 + 512], gm=gm_fn(kc): e.scalar_tensor_tensor(out=tmpA[:], in0=xa, scalar=gm, in1=tmpB[:], op0=ALU.mult, op1=ALU.mult))
                        if sh_fn is None:
                            A(lambda e, o_=out_fn(kc): e.activation(out=o_, in_=tmpA[:], func=AF.Identity))
                        else:
                            A(lambda e, o_=out_fn(kc), b_=sh_fn(kc): e.activation(out=o_, in_=tmpA[:], func=AF.Identity, bias=b_, scale=1.0))

                def layer(l):
                    mark(f'{kind}{l}.ada')
                    mod = modall[:, 1 if S else 0, l, :]
                    gmod = gmodall[:, 1 if S else 0, l, :]
                    mark(f'{kind}{l}.norm')
                    for (t0, _) in tiles:
                        norm_tile(t0, lambda kc: gmod[:, kc:kc + 1], lambda kc: mod[:, kc:kc + 1], lambda kc, t0=t0: hT[:, kc, t0:t0 + 512])
                    DM(xd[:, 0:TT].rearrange("(kc p) t -> p kc t", p=128), X['xT'][:])
                    X['stk'].close()
                    hrhs = lambda kc, t_: hT[:, kc, t_[0]:t_[0] + t_[1]]
                    wi0 = l * 1024

                    mark(f'{kind}{l}.attnproj')
                    with contextlib.ExitStack() as a:
                        qT = t('qT', [128, 4, TT], BF16, a)
                        kT = t('kT', [128, 2, 2, NS, NKC * 128], BF16, a)
                        V(lambda e: e.memset(kT[:], 0.0))
                        vA = t('vA', [128, NS, NKC, 2, 128], BF16, a)
                        V(lambda e: e.memset(vA[:].rearrange("p s c k d -> p (s c k d)"), 1.0))
                        qf = t('qf', [128, 512], F32, a); kst = t('kst', [128, 512], F32, a)
                        ropes = contextlib.ExitStack()
                        if S:
                            rc = t('rc', [128, 2048], F32, ropes); rs_ = t('rs', [128, 2048], F32, ropes); DM(rc[:], I['rc']); DM(rs_[:], I['rs'])

                        def qk_evac(dst_fn, gcol, is_k):
                            def ev(j, t_, ps):
                                t0 = t_[0]
                                A(lambda e: e.activation(out=qf[:], in_=ps[:], func=AF.Identity))
                                V(lambda e: e.tensor_tensor(out=tmpA[:], in0=qf[:], in1=qf[:], op=ALU.mult))
                                p2 = banks[6]
                                T(lambda e: e.matmul(p2[:], lhsT=ob64, rhs=tmpA[:], start=True, stop=True))
                                A(lambda e: e.activation(out=tmpB[:], in_=p2[:], func=AF.Sqrt, bias=epsT[:], scale=1.0))
                                V(lambda e: e.reciprocal(out=tmpB[:], in_=tmpB[:]))
                                V(lambda e: e.scalar_tensor_tensor(out=qf[:], in0=qf[:], scalar=smalls['qk'][:, gcol:gcol + 1], in1=tmpB[:], op0=ALU.mult, op1=ALU.mult))
                                if is_k and not S:
                                    DM(O['nk'][l, j, :, t0:t0 + 512], qf[0:64, :])
                                if S:
                                    p3 = banks[7]
                                    T(lambda e: e.matmul(p3[:], lhsT=swp, rhs=qf[:], start=True, stop=True))
                                    V(lambda e: e.tensor_tensor(out=tmpA[:], in0=qf[:], in1=rc[:, t0:t0 + 512], op=ALU.mult))
                                    V(lambda e: e.tensor_tensor(out=tmpB[:], in0=p3[:], in1=rs_[:, t0:t0 + 512], op=ALU.mult))
                                    for (r_, d_) in dst_fn(j, t0):
                                        V(lambda e, r_=r_, d_=d_: e.tensor_tensor(out=d_, in0=tmpA[r_, :], in1=tmpB[r_, :], op=ALU.add))
                                else:
                                    for (r_, d_) in dst_fn(j, t0):
                                        V(lambda e, r_=r_, d_=d_: e.tensor_copy(out=d_, in_=qf[r_, :]))
                            return ev
                        proj([I['w_in'][l, O_Q // 128 + j] for j in range(4)], 8, hrhs, tiles,
                             qk_evac(lambda j, t0: [(slice(0, 128), qT[:, j, t0:t0 + 512])], l, False))
                        if S:
                            kdst = lambda j, t0: kT[:, j, 0, t0:t0 + 512]
                        else:
                            kdst = lambda j, t0: kT[:, j, :, 0:256]
                        proj([I['w_in'][l, O_K // 128 + j] for j in range(2)], 8, hrhs, tiles,
                             qk_evac((lambda j, t0: [(slice(h_ * 64, h_ * 64 + 64), kT[h_ * 64:h_ * 64 + 64, j, h_, 0, t0:t0 + 512]) for h_ in range(2)]) if S else (lambda j, t0: [(slice(h_ * 64, h_ * 64 + 64), kT[h_ * 64:h_ * 64 + 64, j, h_, :, :].rearrange("p s k -> p (s k)")) for h_ in range(2)]), 4 + l, True))
                        s = wload(I['w_in'][l, O_V // 128], 8)
                        for tk in range(TT // 128):
                            ps = bank()
                            for kc in range(8):
                                T(lambda e, ps=ps, kc=kc, tk=tk, s=s: e.matmul(ps[:, 0:128], lhsT=hT[:, kc, tk * 128:(tk + 1) * 128], rhs=wslots[s][:, kc, :], start=(kc == 0), stop=(kc == 7)))
                            wdone(s)
                            sq_, kc_ = (0, tk) if S else (tk // 2, tk % 2)
                            V(lambda e, ps=ps, sq_=sq_, kc_=kc_: e.tensor_copy(out=vA[:, sq_, kc_, :, 0:64], in_=ps[:, 0:128].rearrange("p (k d) -> p k d", k=2)))
                            if not S:
                                A(lambda e, ps=ps: e.activation(out=kst[:, 0:128], in_=ps[:, 0:128], func=AF.Identity))
                                DM(O['nv'][l, tk * 128:(tk + 1) * 128, :], kst[:, 0:128])
                        if S:
                            for kv in range(2):
                                DM(kst[:, 0:256], I['ck'][l, kv])
                                for h_ in range(2):
                                    V(lambda e, kv=kv, h_=h_: e.tensor_copy(out=kT[h_ * 64:h_ * 64 + 64, kv, h_, 0, 2048:2304], in_=kst[h_ * 64:h_ * 64 + 64, 0:256]))
                            for kc_ in range(2):
                                DM(kst[:, 0:128], I['cv'][l, kc_ * 128:(kc_ + 1) * 128, :])
                                V(lambda e, kc_=kc_: e.tensor_copy(out=vA[:, 0, 16 + kc_, :, 0:64], in_=kst[:, 0:128].rearrange("p (k d) -> p k d", k=2)))
                        proj([I['w_in'][l, O_CG // 128 + j] for j in range(4)], 8, hrhs, tiles,
                             lambda j, t_, ps: A(lambda e: e.activation(out=br[2][:, j, t_[0]:t_[0] + 512], in_=ps[:], func=AF.Silu)))
                        if DEBUG and S and l == DBG_L:
                            with contextlib.ExitStack() as a2:
                                dq = t('dq', [128, 2, 2304], F32, a2)
                                for hf in range(2):
                                    V(lambda e, hf=hf: e.tensor_copy(out=dq[:, :, 0:2048], in_=qT[:, 2 * hf:2 * hf + 2, :]))
                                    DM(O['dbgQ'][:, 2 * hf:2 * hf + 2, :], dq[:, :, 0:2048])
                                V(lambda e: e.tensor_copy(out=dq[:], in_=kT[:, :, 0, 0, :]))
                                DM(O['dbgK'], dq[:])
                                V(lambda e: e.tensor_copy(out=dq[:].rearrange("p a b -> p (a b)"), in_=vA[:, 0].rearrange("p c k d -> p (c k d)")))
                                DM(O['dbgV'].rearrange("p c k d -> p (c k d)"), dq[:].rearrange("p a b -> p (a b)"))
                        ropes.close()
                        mark(f'{kind}{l}.attncore')
                        exalls = [t(f'exall{i}', [128, NKC, 512], BF16, a) for i in range(2)]
                        hcount = [0]
                        for sq_ in range(NS):
                            for (q0, nq) in ([(t0, 512) for t0 in range(0, LS, 512)] if S else [(sq_ * 256, 256)]):
                                for hq in range(8):
                                    kv, qc, r0 = hq // 4, hq // 2, (hq % 2) * 64
                                    hp = hcount[0] % 2; hcount[0] += 1
                                    exall = exalls[hp]
                                    po, pz = banks[4 + 2 * hp], banks[5 + 2 * hp]
                                    for kc_ in range(NKC):
                                        ps = banks[kc_ % 4]
                                        T(lambda e, ps=ps, kc_=kc_, kv=kv, sq_=sq_, qc=qc, r0=r0, q0=q0, nq=nq: e.matmul(ps[:, 0:nq], lhsT=kT[:, kv, r0 // 64, sq_, kc_ * 128:(kc_ + 1) * 128], rhs=qT[:, qc, q0:q0 + nq], start=True, stop=True))
                                        A(lambda e, ps=ps, kc_=kc_, nq=nq: e.activation(out=exall[:, kc_, 0:nq], in_=ps[:, 0:nq], func=AF.Exp, scale=0.125))
                                    for kc_ in range(NKC):
                                        T(lambda e, kc_=kc_, kv=kv, sq_=sq_, nq=nq: e.matmul(po[:, 0:nq], lhsT=vA[:, sq_, kc_, kv, :], rhs=exall[:, kc_, 0:nq], start=(kc_ == 0), stop=(kc_ == NKC - 1)))
                                    A(lambda e: e.activation(out=tmpB[:, 0:nq], in_=po[:, 0:nq], func=AF.Identity))
                                    T(lambda e: e.matmul(pz[:, 0:nq], lhsT=cf[:, 5, :], rhs=tmpB[:, 0:nq], start=True, stop=True))
                                    rs0 = slice(r0, r0 + 64)
                                    den, num = (pz, tmpB) if r0 == 0 else (tmpB, pz)
                                    V(lambda e: e.reciprocal(out=tmpA[rs0, 0:nq], in_=den[rs0, 0:nq]))
                                    V(lambda e: e.tensor_tensor(out=tmpA[rs0, 0:nq], in0=num[rs0, 0:nq], in1=tmpA[rs0, 0:nq], op=ALU.mult))
                                    V(lambda e: e.tensor_tensor(out=br[2][rs0, qc, q0:q0 + nq], in0=tmpA[rs0, 0:nq], in1=br[2][rs0, qc, q0:q0 + nq], op=ALU.mult))

                    mark(f'{kind}{l}.fft')
                    with contextlib.ExitStack() as a:
                        fT = t('fT', [128, TT // 128, 512], BF16, a)
                        zT = t('zT', [128, 4, TT], BF16, a)
                        OW = 256
                        pq = t('pq', [128, 2, OW], BF16, a)
                        if S:
                            tabs = [t(f'tab{i}', [128, 2, 16, OW], BF16, a) for i in range(2)]

                            def tab_load(oc):
                                for tr in range(2):
                                    DM(tabs[oc % 2][:, tr], I['dpS'][tr, :, oc * OW:(oc + 1) * OW].rearrange("(c p) k -> p c k", p=128))
                            tab_load(0)
                        for j in range(4):
                            s = wload(I['w_in'][l, O_FIN // 128 + j], 8)
                            for tk in range(TT // 128):
                                ps = bank()
                                for kc in range(8):
                                    T(lambda e, ps=ps, kc=kc, tk=tk, s=s: e.matmul(ps[:, 0:128], lhsT=hT[:, kc, tk * 128:(tk + 1) * 128], rhs=wslots[s][:, kc, :], start=(kc == 0), stop=(kc == 7)))
                                wdone(s)
                                A(lambda e, ps=ps, tk=tk, j=j: e.activation(out=fT[:, tk, j * 128:(j + 1) * 128], in_=ps[:, 0:128], func=AF.Identity))
                        NIC = LS // 128
                        for sq_ in range(NS):
                            for oc in range(LS // OW):
                                if S:
                                    if oc + 1 < LS // OW:
                                        tab_load(oc + 1)
                                    tb = lambda tr, ic, tab=tabs[oc % 2]: tab[:, tr, ic, :]
                                else:
                                    tb = lambda tr, ic: dpP[:, tr, ic, :]
                                for g_ in range(4):
                                    pp = [bank(), bank()]
                                    for tr in range(2):
                                        for ic in range(NIC):
                                            T(lambda e, tr=tr, ic=ic, g_=g_, sq_=sq_, pp=pp, tb=tb: e.matmul(pp[tr][:, 0:OW], lhsT=fT[:, sq_ * NIC + ic, g_ * 128:(g_ + 1) * 128], rhs=tb(tr, ic), start=(ic == 0), stop=(ic == NIC - 1)))
                                    A(lambda e, pp=pp: e.activation(out=pq[:, 0, :], in_=pp[0][:, 0:OW], func=AF.Identity))
                                    V(lambda e, pp=pp: e.tensor_copy(out=pq[:, 1, :], in_=pp[1][:, 0:OW]))
                                    pz = bank()
                                    T(lambda e, pz=pz: e.matmul(pz[:, 0:OW], lhsT=dcc[:, 0, :], rhs=pq[:, 0, :], start=True, stop=False))
                                    T(lambda e, pz=pz: e.matmul(pz[:, 0:OW], lhsT=dcc[:, 1, :], rhs=pq[:, 1, :], start=False, stop=True))
                                    c0 = sq_ * LS + oc * OW
                                    A(lambda e, pz=pz, g_=g_, c0=c0: e.activation(out=zT[:, g_, c0:c0 + OW], in_=pz[:, 0:OW], func=AF.Identity))
                        proj([I['w_in'][l, O_FG // 128 + j] for j in range(4)], 8, hrhs, tiles,
                             lambda j, t_, ps: A(lambda e: e.activation(out=br[1][:, j, t_[0]:t_[0] + 512], in_=ps[:], func=AF.Silu)))
                        proj([I['fnet_w'][l, j] for j in range(4)], 4,
                             lambda kc, t_: zT[:, kc, t_[0]:t_[0] + t_[1]], tiles,
                             lambda j, t_, ps: V(lambda e: e.tensor_tensor(out=br[1][:, j, t_[0]:t_[0] + 512], in0=ps[:], in1=br[1][:, j, t_[0]:t_[0] + 512], op=ALU.mult)))

                    mark(f'{kind}{l}.s5front')
                    with contextlib.ExitStack() as a:
                        U8 = t('U8', [128, 32, C], BF16, a)
                        aT = br[0]
                        proj([I['w_in'][l, O_AIN // 128 + j] for j in range(4)], 8, hrhs, tiles,
                             lambda j, t_, ps: A(lambda e: e.activation(out=aT[:, j, t_[0]:t_[0] + 512], in_=ps[:], func=AF.Identity)))
                        with contextlib.ExitStack() as b0:
                            sel = t('sel', [128, 64, 128], BF16, b0); DM(sel[:], I['sel'])
                            for g_ in range(32):
                                ps = bank()
                                for s_ in range(8):
                                    T(lambda e, ps=ps, g_=g_, s_=s_, sel=sel: e.matmul(ps[:, 0:C], lhsT=sel[:, (g_ % 8) * 8 + s_, :], rhs=aT[:, g_ // 8, s_:TT:8], start=(s_ == 0), stop=(s_ == 7)))
                                A(lambda e, ps=ps, g_=g_: e.activation(out=U8[:, g_, :], in_=ps[:, 0:C], func=AF.Identity))

                        GP = 16 if S else 32

                        def s5_pass(gq):
                            with contextlib.ExitStack() as b:
                                g0 = gq * GP
                                Mw = t('Mw', [128, GP, 2, 2, 64], BF16, b); Nw = t('Nw', [128, GP, 2, 128], BF16, b); Tw = t('Tw', [128, GP, 128], BF16, b)
                                DM(Mw[:], opM[l][:, g0:g0 + GP]); DM(Tw[:], opT[l][:, g0:g0 + GP, :])
                                PWt = t('PWt', [128, 2, GP, 16], F32, b); AAt = t('AAt', [128, 2, 2, GP], F32, b)
                                for d in range(2):
                                    hs = slice(d * 64, d * 64 + 64)
                                    DM(Nw[hs], opN[l][:, g0:g0 + GP, d])
                                    DM(PWt[hs], opP[l][:, :, d * 32 + g0:d * 32 + g0 + GP, :])
                                    DM(AAt[hs], opA[l][:, :, :, d * 32 + g0:d * 32 + g0 + GP])
                                H = t('H', [128, 2, GP, NS, CS + 2], F32, b)
                                Hb = t('Hb', [128, 2, GP, NS, CS], BF16, b)
                                fin = t('fin', [128, 2, GP, NS], F32, b)
                                w1 = t('w1', [128, 2, GP, NB], F32, b); w2 = t('w2', [128, 2, GP, NB], F32, b)
                                s1 = w1[:].rearrange("p a g b -> p (a g b)")[:, 0:2 * NB * 16].rearrange("p (g b j) -> p g b j", g=2, b=NB)
                                s2 = w2[:].rearrange("p a g b -> p (a g b)")[:, 0:2 * NB * 16].rearrange("p (g b j) -> p g b j", g=2, b=NB)
                                V(lambda e: e.memset(H[:].rearrange("p r g s c -> p (r g s c)"), 0.0))
                                if S:
                                    stt_ = t('stt', [128, 2, GP], F32, b)
                                    for d in range(2):
                                        DM(stt_[d * 64:d * 64 + 64], I['st'][l][:, :, d * 32 + g0:d * 32 + g0 + GP])
                                    V(lambda e: e.tensor_copy(out=H[:, :, :, 0, 0], in_=stt_[:]))
                                for g_ in range(GP):
                                    for ri in range(2):
                                        ps = bank()
                                        T(lambda e: e.matmul(ps[:, 0:C], lhsT=Mw[:, g_, ri].rearrange("p d n -> p (d n)"), rhs=U8[:, g0 + g_, :], start=True, stop=True))
                                        V(lambda e: e.tensor_copy(out=H[0:64, ri, g_, :, 1:CS + 1], in_=ps[0:64, 0:C].rearrange("p (s c) -> p s c", s=NS)))
                                        A(lambda e: e.activation(out=H[64:128, ri, g_, :, CS:0:-1], in_=ps[64:128, 0:C].rearrange("p (s c) -> p s c", s=NS), func=AF.Identity))
                                AS = t('AS', [128, 2, 2, GP], F32, b)
                                V(lambda e: e.tensor_scalar(out=AS[:, 0], in0=AAt[:, 1], scalar1=-1.0, scalar2=None, op0=ALU.mult))
                                V(lambda e: e.tensor_copy(out=AS[:, 1], in_=AAt[:, 1]))
                                for sq_ in range(NS):
                                    Vall = H[:, :, :, sq_, 1:CS + 1].rearrange("p r g (b j) -> p r g b j", j=16)
                                    arb = AAt[:, 0, 0, :].unsqueeze(1).unsqueeze(3).broadcast_to([128, 2, GP, NB])
                                    asb = AS[:, :, 0, :].unsqueeze(3).broadcast_to([128, 2, GP, NB])
                                    for j in range(1, 16):
                                        cur = Vall[:, :, :, :, j]; prev = Vall[:, :, :, :, j - 1]; psw = Vall[:, ::-1, :, :, j - 1]
                                        V(lambda e: e.tensor_tensor(out=w1[:], in0=prev, in1=arb, op=ALU.mult))
                                        V(lambda e: e.tensor_tensor(out=w2[:], in0=psw, in1=asb, op=ALU.mult))
                                        V(lambda e: e.tensor_tensor(out=cur, in0=cur, in1=w1[:], op=ALU.add))
                                        V(lambda e: e.tensor_tensor(out=cur, in0=cur, in1=w2[:], op=ALU.add))
                                a16b = AAt[:, 0, 1, :].unsqueeze(1).unsqueeze(3).broadcast_to([128, 2, GP, NS])
                                as16 = AS[:, :, 1, :].unsqueeze(3).broadcast_to([128, 2, GP, NS])
                                for bq in range(NB):
                                    dst, src = 16 * (bq + 1), 16 * bq
                                    cur = H[:, :, :, :, dst]; prev = H[:, :, :, :, src]; psw = H[:, ::-1, :, :, src]
                                    V(lambda e: e.tensor_tensor(out=w1[:, :, :, 0:NS], in0=prev, in1=a16b, op=ALU.mult))
                                    V(lambda e: e.tensor_tensor(out=w2[:, :, :, 0:NS], in0=psw, in1=as16, op=ALU.mult))
                                    V(lambda e: e.tensor_tensor(out=cur, in0=cur, in1=w1[:, :, :, 0:NS], op=ALU.add))
                                    V(lambda e: e.tensor_tensor(out=cur, in0=cur, in1=w2[:, :, :, 0:NS], op=ALU.add))
                                for gb_ in range(0, GP, 2):
                                    for sq_ in range(NS):
                                        car_r = H[:, 0, gb_:gb_ + 2, sq_, 0:CS:16]; car_i = H[:, 1, gb_:gb_ + 2, sq_, 0:CS:16]
                                        cb = lambda x: x.unsqueeze(3).broadcast_to([128, 2, NB, 15])
                                        pb_ = lambda x: x.unsqueeze(2).broadcast_to([128, 2, NB, 15])
                                        pr = pb_(PWt[:, 0, gb_:gb_ + 2, 0:15]); pi_ = pb_(PWt[:, 1, gb_:gb_ + 2, 0:15])
                                        cr_, ci_ = cb(car_r), cb(car_i)
                                        u1 = s1[:, :, :, 0:15]; u2 = s2[:, :, :, 0:15]
                                        vr_ = H[:, 0, gb_:gb_ + 2, sq_, 1:CS + 1].rearrange("p g (b j) -> p g b j", j=16)[:, :, :, 0:15]
                                        vi_ = H[:, 1, gb_:gb_ + 2, sq_, 1:CS + 1].rearrange("p g (b j) -> p g b j", j=16)[:, :, :, 0:15]
                                        V(lambda e: e.tensor_tensor(out=u1, in0=cr_, in1=pr, op=ALU.mult))
                                        V(lambda e: e.tensor_tensor(out=u2, in0=ci_, in1=pi_, op=ALU.mult))
                                        V(lambda e: e.tensor_tensor(out=u1, in0=u1, in1=u2, op=ALU.subtract))
                                        V(lambda e: e.tensor_tensor(out=vr_, in0=vr_, in1=u1, op=ALU.add))
                                        V(lambda e: e.tensor_tensor(out=u1, in0=cr_, in1=pi_, op=ALU.mult))
                                        V(lambda e: e.tensor_tensor(out=u2, in0=ci_, in1=pr, op=ALU.mult))
                                        V(lambda e: e.tensor_tensor(out=u1, in0=u1, in1=u2, op=ALU.add))
                                        V(lambda e: e.tensor_tensor(out=vi_, in0=vi_, in1=u1, op=ALU.add))
                                for ri in range(2):
                                    A(lambda e: e.activation(out=Hb[0:64, ri], in_=H[0:64, ri, :, :, 0:CS], func=AF.Identity))
                                    V(lambda e: e.tensor_copy(out=Hb[64:128, ri], in_=H[64:128, ri, :, :, CS - 1::-1]))
                                if not S:
                                    V(lambda e: e.tensor_copy(out=fin[:], in_=H[:, :, :, :, CS]))
                                    for d in range(2):
                                        DM(O['nst'][l, :, :, d, g0:g0 + GP, :], fin[d * 64:d * 64 + 64])
                                for g_ in range(GP):
                                    gg = g0 + g_
                                    ps = bank()
                                    T(lambda e: e.matmul(ps[:, 0:C], lhsT=Tw[:, g_, :], rhs=U8[:, gg, :], start=True, stop=False))
                                    for ri in range(2):
                                        T(lambda e: e.matmul(ps[:, 0:C], lhsT=Nw[:, g_, ri, :], rhs=Hb[:, ri, g_].rearrange("p s c -> p (s c)"), start=False, stop=(ri == 1)))
                                    A(lambda e: e.activation(out=U8[:, gg, :], in_=ps[:, 0:C], func=AF.Identity))
                        mark(f'{kind}{l}.s5scan')
                        for gq in range(32 // GP):
                            s5_pass(gq)
                        mark(f'{kind}{l}.s5back')
                        with contextlib.ExitStack() as b0:
                            sel = t('sel', [128, 64, 128], BF16, b0); DM(sel[:], I['sel'])
                            for blk in range(4):
                                for half in range(2):
                                    pss = [bank(), bank()]
                                    for tq in range(4):
                                        tp = half * 4 + tq
                                        pv = pss[tq // 2][:, (tq % 2) * C:(tq % 2) * C + C]
                                        for gi in range(8):
                                            T(lambda e, pv=pv, tp=tp, gi=gi, blk=blk, sel=sel: e.matmul(pv, lhsT=sel[:, tp * 8 + gi, :], rhs=U8[:, blk * 8 + gi, :], start=(gi == 0), stop=(gi == 7)))
                                        A(lambda e, pv=pv, tp=tp, blk=blk: e.activation(out=aT[:, blk, tp:TT:8], in_=pv, func=AF.Identity))
                        for j in range(4):
                            for (t0, _) in tiles:
                                yv = aT[:, j, t0:t0 + 512]
                                V(lambda e, yv=yv: e.tensor_tensor(out=tmpA[:], in0=yv, in1=yv, op=ALU.mult))
                                V(lambda e: e.tensor_scalar(out=tmpA[:], in0=tmpA[:], scalar1=0.044715, scalar2=1.0, op0=ALU.mult, op1=ALU.add))
                                V(lambda e, yv=yv: e.tensor_tensor(out=tmpA[:], in0=tmpA[:], in1=yv, op=ALU.mult))
                                A(lambda e: e.activation(out=tmpB[:], in_=tmpA[:], func=AF.Sigmoid, scale=1.5957691216057308))
                                V(lambda e, yv=yv: e.tensor_tensor(out=yv, in0=tmpB[:], in1=yv, op=ALU.mult))
                        G4 = U8[:].rearrange("p g c -> p (g c)").rearrange("p (j t) -> p j t", j=4)
                        for j in range(4):
                            sv = wload(I['glu_w'][l, j], 4)
                            sg_ = wload(I['glu_w'][l, 4 + j], 4)
                            for (t0, _) in tiles:
                                pv_, pg_ = bank(), bank()
                                for kc in range(4):
                                    T(lambda e, kc=kc, pv_=pv_, t0=t0, sv=sv: e.matmul(pv_[:], lhsT=wslots[sv][:, kc, :], rhs=aT[:, kc, t0:t0 + 512], start=(kc == 0), stop=(kc == 3)))
                                for kc in range(4):
                                    T(lambda e, kc=kc, pg_=pg_, t0=t0, sg_=sg_: e.matmul(pg_[:], lhsT=wslots[sg_][:, kc, :], rhs=aT[:, kc, t0:t0 + 512], start=(kc == 0), stop=(kc == 3)))
                                wdone(sv); wdone(sg_)
                                A(lambda e, pg_=pg_: e.activation(out=tmpA[:], in_=pg_[:], func=AF.Sigmoid))
                                V(lambda e, pv_=pv_, j=j, t0=t0: e.tensor_tensor(out=G4[:, j, t0:t0 + 512], in0=pv_[:], in1=tmpA[:], op=ALU.mult))

                        def ag_evac(j, t_, ps):
                            A(lambda e: e.activation(out=tmpA[:], in_=ps[:], func=AF.Silu))
                            V(lambda e: e.tensor_tensor(out=br[0][:, j, t_[0]:t_[0] + 512], in0=tmpA[:], in1=G4[:, j, t_[0]:t_[0] + 512], op=ALU.mult))
                        proj([I['w_in'][l, O_AG // 128 + j] for j in range(4)], 8, hrhs, tiles, ag_evac)

                    if DEBUG and l == DBG_L:
                        with contextlib.ExitStack() as a:
                            dt_ = t('dbgt', [128, 4, TT], F32, a)
                            for n in range(3):
                                V(lambda e, n=n: e.tensor_copy(out=dt_[:], in_=br[n][:]))
                                DM(O['dbgS' if S else 'dbgP'][n], dt_[:])
                    mark(f'{kind}{l}.phaseC')
                    X['stk'] = contextlib.ExitStack()
                    X['xT'] = xT = t('xT', [128, 8, TT], F32, X['stk'])
                    DM(xT[:], xd[:, 0:TT].rearrange("(kc p) t -> p kc t", p=128))
                    with contextlib.ExitStack() as a:
                        mj = t('mj', [128, TT], F32, a); mjb = t('mjb', [128, TT], BF16, a)
                        for j in range(8):
                            for n in range(3):
                                sm = wload(I['merge_w'][l, n * 8 + j], 8)
                                sw_ = wload(I['w_br'][l * 3 + n, j], 4)
                                for (t0, _) in tiles:
                                    pg_, pp_ = bank(), bank()
                                    for kc in range(8):
                                        T(lambda e, kc=kc, pg_=pg_, t0=t0, sm=sm: e.matmul(pg_[:], lhsT=wslots[sm][:, kc, :], rhs=hT[:, kc, t0:t0 + 512], start=(kc == 0), stop=(kc == 7)))
                                    for kc in range(4):
                                        T(lambda e, kc=kc, pp_=pp_, t0=t0, sw_=sw_, n=n: e.matmul(pp_[:], lhsT=wslots[sw_][:, kc, :], rhs=br[n][:, kc, t0:t0 + 512], start=(kc == 0), stop=(kc == 3)))
                                    wdone(sm); wdone(sw_)
                                    bcol = l * 24 + n * 8 + j
                                    A(lambda e, pg_=pg_, bcol=bcol: e.activation(out=tmpA[:], in_=pg_[:], func=AF.Sigmoid, bias=smalls['mb'][:, bcol:bcol + 1], scale=1.0))
                                    if n == 0:
                                        V(lambda e, pp_=pp_, t0=t0: e.tensor_tensor(out=mj[:, t0:t0 + 512], in0=pp_[:], in1=tmpA[:], op=ALU.mult))
                                    else:
                                        V(lambda e, pp_=pp_: e.tensor_tensor(out=tmpA[:], in0=pp_[:], in1=tmpA[:], op=ALU.mult))
                                        V(lambda e, t0=t0: e.tensor_tensor(out=mj[:, t0:t0 + 512], in0=mj[:, t0:t0 + 512], in1=tmpA[:], op=ALU.add))
                            V(lambda e: e.tensor_copy(out=mjb[:], in_=mj[:]))
                            so = wload(I['w_out'][l * 1024 + j * 128:l * 1024 + (j + 1) * 128, :].rearrange("p (o n) -> p o n", n=128), 8)
                            for (t0, _) in tiles:
                                for o in range(8):
                                    ps = bank()
                                    T(lambda e, ps=ps, o=o, t0=t0, so=so: e.matmul(ps[:], lhsT=wslots[so][:, o, :], rhs=mjb[:, t0:t0 + 512], start=True, stop=True))
                                    V(lambda e, ps=ps, o=o, xa=xT[:, o, t0:t0 + 512]: e.scalar_tensor_tensor(out=xa, in0=ps[:], scalar=mod[:, 16 + o:17 + o], in1=xa, op0=ALU.mult, op1=ALU.add))
                            wdone(so)
                for l_ in range(n_layers):
                    layer(l_)
                mark(f'{kind}.final')
                if S:
                    yst = br[0][:].bitcast(F32).rearrange("p a b -> p (a b)").rearrange("p (k t) -> p k t", k=8)
                else:
                    yst = yst_t[:]
                for (t0, _) in tiles:
                    norm_tile(t0, lambda kc: smalls['fg'][:, kc:kc + 1], None, lambda kc: yst[:, kc, :])
                    DM(yout[:, t0:t0 + 512].rearrange("(kc p) t -> p kc t", p=128), yst)
                X['stk'].close()

        job('P')
        job('S')
        ch.barrier()
        ch.emit('dq', lambda e: e.dma_start(out=xd[0:1, 0:8], in_=xd[1:2, 0:8]), inc=16, queue='act')
        fin = ch.cnt['dq']
        with nc.Block() as block:
            @block.tensor
            def _(e):
                for f in ch.prog['pe']: f(e)
            @block.scalar
            def _(e):
                for f in ch.prog['act']: f(e)
                e.wait_ge(sems['dq'], fin)
            @block.vector
            def _(e):
                for f in ch.prog['dve']: f(e)
            @block.gpsimd
            def _(e):
                for f in pool_prog: f(e)
            @block.sync
            def _(e):
                for f in sync_prog: f(e)
    return nc


def _consts():
    bf = ml_dtypes.bfloat16
    c = {}
    cf = np.zeros((128, 6, 128), np.float32)
    cf[:, 0, :] = np.eye(128)
    sw = np.zeros((128, 128), np.float32)
    for h in range(2):
        for i in range(32):
            sw[h * 64 + 32 + i, h * 64 + i] = 1.0
            sw[h * 64 + i, h * 64 + 32 + i] = 1.0
    cf[:, 1, :] = sw
    ob = np.zeros((128, 128), np.float32); ob[0:64, 0:64] = 1 / 64.; ob[64:, 64:] = 1 / 64.
    cf[:, 2, :] = ob; cf[:, 3, :] = 1 / 1024.; cf[:, 4, :] = 1.0
    hsw = np.zeros((128, 128), np.float32); hsw[np.arange(64) + 64, np.arange(64)] = 1.0; hsw[np.arange(64), np.arange(64) + 64] = 1.0
    cf[:, 5, :] = hsw
    c['cf'] = cf
    sel = np.zeros((128, 64, 128), np.float32)
    for a in range(8):
        for b in range(8):
            for i in range(16):
                sel[16 * a + i, a * 8 + b, 16 * b + i] = 1.0
    c['sel'] = sel.astype(bf)
    mk = np.zeros((128, 2, 128), np.float32)
    for s in range(8):
        for t in range(8):
            if s <= t: mk[16 * s:16 * s + 16, 0, 16 * t:16 * t + 16] = 1.0
            if s >= t: mk[16 * s:16 * s + 16, 1, 16 * t:16 * t + 16] = 1.0
    c['mk'] = mk
    pos = np.arange(2048); row = (pos // 64).astype(np.float64); col = (pos % 64).astype(np.float64)
    inv = 10000.0 ** (-np.arange(16, dtype=np.float64) / 16)
    ang = np.concatenate([row[:, None] * inv, col[:, None] * inv], axis=-1).astype(np.float32)
    cos, sin = np.cos(ang).T, np.sin(ang).T
    rc = np.zeros((128, 2048), np.float32); rs = np.zeros((128, 2048), np.float32)
    for h in range(2):
        rc[h * 64:h * 64 + 32] = cos; rc[h * 64 + 32:h * 64 + 64] = cos
        rs[h * 64:h * 64 + 32] = -sin; rs[h * 64 + 32:h * 64 + 64] = sin
    c['rc'] = rc; c['rs'] = rs
    k = np.arange(128)
    a = 2 * np.pi * np.outer(k, k) / 128
    c['dcc'] = np.stack([np.cos(a), -np.sin(a)], axis=1).astype(np.float32).astype(bf)
    for nm, n in (('dpP', 256), ('dpS', 2048)):
        k = np.arange(n, dtype=np.int64)
        a = 2 * np.pi * (np.outer(k, k) % n).astype(np.float64) / n
        sc = 1.0 / math.sqrt(n * 128)
        c[nm] = np.stack([np.cos(a) * sc, np.sin(a) * sc]).astype(np.float32).astype(bf)
    return c


_NC = {}


def kernel(x_prompt, x_sample, cache_k, cache_v, state_ssm, c, c_ctx,
           norm_g, ada_w, ada_b, w_in, ssm_a_re, ssm_a_im, ssm_log_dt,
           ssm_b_re, ssm_b_im, ssm_c_re, ssm_c_im, ssm_d, glu_w, fnet_w,
           q_gain, k_gain, w_branch, merge_w, merge_b, w_out, final_g):
    f = lambda a: np.ascontiguousarray(np.asarray(a, dtype=np.float32))
    x_prompt, x_sample, cache_k, cache_v, state_ssm = map(f, (x_prompt, x_sample, cache_k, cache_v, state_ssm))
    vec = lambda v: f(np.asarray(v, np.float32).reshape(-1, 128).T)
    shared = _consts()
    shared['ng'] = f(np.concatenate([vec(norm_g[l]) for l in range(NL)], axis=1))
    shared['fg'] = vec(final_g)
    shared['adab'] = f(np.concatenate([vec(ada_b[l]) for l in range(NL)], axis=1))
    shared['mb'] = f(np.concatenate([vec(merge_b[l]) for l in range(NL)], axis=1))
    qg = np.asarray(q_gain, np.float32)[:, PERM]; kg = np.asarray(k_gain, np.float32)[:, PERM]
    shared['qk'] = f(np.concatenate([np.tile(qg.T, (2, 1)), np.tile(kg.T, (2, 1))], axis=1))
    sd = np.asarray(ssm_d, np.float32).reshape(NL, 32, 16)
    shared['dsk'] = f(np.tile(sd.transpose(2, 0, 1).reshape(16, NL * 32), (8, 1)))
    wi = np.asarray(w_in, np.float32)
    qcols = (2048 + (np.arange(8)[:, None] * 64 + PERM[None, :])).reshape(-1)
    k0 = 2560 + PERM; k1 = 2560 + 64 + PERM
    cols = np.concatenate([np.arange(0, 2048), qcols, k0, k0, k1, k1, np.arange(2688, 2816), np.arange(2816, 3328)])
    relay = lambda W, B, K, N: f(np.asarray(W, np.float32).reshape(B, K // 128, 128, N // 128, 128).transpose(0, 3, 2, 1, 4))
    shared['w_in'] = relay(wi[:, :, cols], NL, 1024, W_INX)
    shared['ada_w'] = relay(ada_w, NL, 1024, 3072)
    shared['glu_w'] = relay(glu_w, NL, 512, 1024)
    shared['fnet_w'] = relay(fnet_w, NL, 512, 512)
    shared['w_br'] = relay(np.asarray(w_branch, np.float32).reshape(NL * 3, 512, 1024), NL * 3, 512, 1024)
    shared['merge_w'] = relay(merge_w, NL, 1024, 3072)
    shared['w_out'] = f(np.asarray(w_out, np.float32).reshape(NL * 1024, 1024))
    are = np.asarray(ssm_a_re, np.float32).reshape(NL, 64, 64); aim = np.asarray(ssm_a_im, np.float32).reshape(NL, 64, 64)
    shared['sa'] = f(np.stack([are.transpose(0, 2, 1), aim.transpose(0, 2, 1)], axis=2))
    shared['sdt'] = f(np.broadcast_to(np.asarray(ssm_log_dt, np.float32).reshape(NL, 1, 64), (NL, 64, 64)))
    bre = np.asarray(ssm_b_re, np.float32).reshape(NL, 64, 64, 16); bim = np.asarray(ssm_b_im, np.float32).reshape(NL, 64, 64, 16)
    shared['sb'] = f(np.stack([bre.transpose(0, 2, 1, 3), bim.transpose(0, 2, 1, 3)], axis=2))
    cre = np.asarray(ssm_c_re, np.float32).reshape(NL, 64, 16, 64); cim = np.asarray(ssm_c_im, np.float32).reshape(NL, 64, 16, 64)
    shared['sc'] = f(np.stack([cre.transpose(0, 3, 1, 2), cim.transpose(0, 3, 1, 2)], axis=2))
    in_maps = []
    for core in range(8):
        m = dict(shared)
        sbi = core % 2
        m['xp'] = f(x_prompt[2 * core:2 * core + 2].reshape(512, 1024).T)
        m['xs'] = f(x_sample[sbi].T)
        ckp = cache_k[sbi][:, :, :, PERM]
        ckt = ckp.transpose(0, 2, 3, 1)
        m['ck'] = f(np.concatenate([ckt, ckt], axis=2))
        m['cv'] = f(cache_v[sbi].reshape(NL, 256, 128))
        stt = state_ssm[sbi]
        m['st'] = f(stt.transpose(0, 3, 4, 1, 2).reshape(NL, 64, 2, 64))
        m['cc'] = f(np.concatenate([vec(c_ctx), vec(np.asarray(c, np.float32)[sbi])], axis=1))
        in_maps.append(m)
    if 'nc' not in _NC:
        _NC['nc'] = build()
    res = run_bass_kernel_spmd(_NC['nc'], in_maps, core_ids=list(range(8)))
    R = res.results
    if DEBUG:
        DBG['P'] = [r['dbgP'] for r in R]; DBG['S'] = [r['dbgS'] for r in R]
        DBG['Q'] = R[0]['dbgQ']; DBG['K'] = R[0]['dbgK']; DBG['V'] = R[0]['dbgV']
    y_prompt = np.zeros((16, 256, 1024), np.float32); y_sample = np.zeros((2, 2048, 1024), np.float32)
    nk = np.zeros((16, NL, 256, 2, 64), np.float32); nv = np.zeros((16, NL, 256, 2, 64), np.float32)
    ns = np.zeros((16, NL, 2, 32, 64, 2), np.float32)
    inv = np.argsort(PERM)
    for core in range(8):
        r = R[core]
        y_prompt[2 * core:2 * core + 2] = r['yp'].T.reshape(2, 256, 1024)
        if core < 2:
            y_sample[core] = r['ys'].T
        k_ = r['nk'].reshape(NL, 2, 64, 2, 256)[:, :, inv]
        nk[2 * core:2 * core + 2] = k_.transpose(3, 0, 4, 1, 2)
        v_ = r['nv'].reshape(NL, 2, 256, 2, 64)
        nv[2 * core:2 * core + 2] = v_.transpose(1, 0, 2, 3, 4)
        s_ = r['nst']
        ns[2 * core:2 * core + 2] = s_.transpose(5, 0, 3, 4, 1, 2)
    return (y_prompt, y_sample, nk, nv, ns)
```
